# Optimizing a Trainium2 kernel written in Bass

```python
import math
import jax, jax.numpy as jnp
from jax import lax
import numpy as np

D_MODEL = 2048
BATCH = 2
SEQ = 8192
DEPTH = 4
DEC_BATCH = 8
DEC_SEQ = 4096
PAST_LEN = 128

HEAD_DIM = 128
SCALE = HEAD_DIM ** -0.5
A_HEADS = 6
A_KV_HEADS = 2
B_HEADS = 4
B_KV_HEADS = 2
B_WINDOW = 128
C_PATTERNS = ((128, 1), (512, 4), (2048, 16))
C_GROUPS = len(C_PATTERNS)
C_HEADS_PER_GROUP = 2
C_HEADS = C_GROUPS * C_HEADS_PER_GROUP
C_KV_HEADS = C_GROUPS
MIX_WIDTH = (A_HEADS + B_HEADS + C_HEADS) * HEAD_DIM
Q_BLOCK = 128
GRID_W = 64
ROPE_THETA = 10000.0
ROPE_AXIS_DIM = HEAD_DIM // 2
REL_BUCKETS = 32
REL_MAX_DIST = 1024
REL_HEADS = B_HEADS + C_HEADS
MEM_LEN = 256
X_HEADS = 4
X_WIDTH = X_HEADS * HEAD_DIM
D_FF = ((8 * D_MODEL + 3 * 256 - 1) // (3 * 256)) * 256
RMS_EPS = 1e-6
NEG_INF = -1e30
PROJ_SIZES = (A_HEADS * HEAD_DIM, A_KV_HEADS * HEAD_DIM, A_KV_HEADS * HEAD_DIM,
              B_HEADS * HEAD_DIM, B_KV_HEADS * HEAD_DIM, B_KV_HEADS * HEAD_DIM,
              C_HEADS * HEAD_DIM, C_KV_HEADS * HEAD_DIM, C_KV_HEADS * HEAD_DIM)
PROJ_WIDTH = sum(PROJ_SIZES)

kernel_name = 'hybrid_parallel_encoder'


def rmsnorm(x, g):
    xf = x.astype(jnp.float32)
    y = xf * lax.rsqrt(jnp.mean(xf * xf, axis=-1, keepdims=True) + RMS_EPS)
    return (y * g.astype(jnp.float32)).astype(x.dtype)


def axial_rope_tables(seq_len):
    rows = seq_len // GRID_W
    row = jnp.repeat(jnp.arange(rows), GRID_W).astype(jnp.float32)
    col = jnp.tile(jnp.arange(GRID_W), rows).astype(jnp.float32)
    inv = ROPE_THETA ** (-jnp.arange(0, ROPE_AXIS_DIM, 2, dtype=jnp.float32) / ROPE_AXIS_DIM)
    ang_r = row[:, None] * inv
    ang_c = col[:, None] * inv
    return (jnp.cos(ang_r), jnp.sin(ang_r), jnp.cos(ang_c), jnp.sin(ang_c))


def _rotate(u, c, s):
    u1, u2 = jnp.split(u, 2, axis=-1)
    c = c[:, None]
    s = s[:, None]
    return jnp.concatenate([u1 * c - u2 * s, u1 * s + u2 * c], axis=-1)


def apply_axial_rope(x, cos_r, sin_r, cos_c, sin_c):
    xf = x.astype(jnp.float32)
    xr, xc = jnp.split(xf, 2, axis=-1)
    out = jnp.concatenate([_rotate(xr, cos_r, sin_r), _rotate(xc, cos_c, sin_c)], axis=-1)
    return out.astype(x.dtype)


def t5_bucket(rel):
    nb = REL_BUCKETS // 2
    max_exact = nb // 2
    ret = jnp.where(rel > 0, nb, 0)
    n = jnp.abs(rel)
    large = max_exact + (jnp.log(jnp.maximum(n, 1).astype(jnp.float32) / max_exact)
                         / math.log(REL_MAX_DIST / max_exact) * (nb - max_exact)).astype(jnp.int32)
    large = jnp.minimum(large, nb - 1)
    return ret + jnp.where(n < max_exact, n, large)


def position_bias(rel_bias, rel, col0, n_heads):
    table = rel_bias[:, col0:col0 + n_heads].astype(jnp.float32)
    return jnp.moveaxis(table[t5_bucket(rel)], -1, 0)


def band_offsets(blk):
    return (jnp.arange(3 * blk) - blk)[None, :] - jnp.arange(blk)[:, None]


def to_residue(t, d):
    bt, s = t.shape[0], t.shape[1]
    rest = t.shape[2:]
    return jnp.moveaxis(t.reshape(bt, s // d, d, *rest), 2, 1).reshape(bt * d, s // d, *rest)


def from_residue(t, d, bt):
    l = t.shape[1]
    rest = t.shape[2:]
    return jnp.moveaxis(t.reshape(bt, d, l, *rest), 1, 2).reshape(bt, l * d, *rest)


def dense_attention_blocks(q, k, v):
    bt, s, h, d = q.shape
    hkv = k.shape[2]
    g = h // hkv
    nb = s // Q_BLOCK
    qb = jnp.moveaxis(q.reshape(bt, nb, Q_BLOCK, hkv, g, d), 1, 0)

    def attend(qi):
        sc = jnp.einsum('bqkgd,bskd->bkgqs', qi, k, preferred_element_type=jnp.float32)
        p = jax.nn.softmax(sc, axis=-1).astype(v.dtype)
        return jnp.einsum('bkgqs,bskd->bqkgd', p, v, preferred_element_type=jnp.float32).astype(q.dtype)

    o = lax.map(attend, qb)
    return jnp.moveaxis(o, 0, 1).reshape(bt, s, h, d)


def banded_attention(q, k, v, half_window, bias, sink):
    bt, l, h, d = q.shape
    hkv = k.shape[2]
    g = h // hkv
    blk = half_window
    nb = -(-l // blk)
    pad = nb * blk - l
    qb = jnp.pad(q, ((0, 0), (0, pad), (0, 0), (0, 0))).reshape(bt, nb, blk, hkv, g, d)

    def windows(t):
        tp = jnp.pad(t, ((0, 0), (blk, blk + pad), (0, 0), (0, 0))).reshape(bt, nb + 2, blk, hkv, d)
        return jnp.concatenate([tp[:, :-2], tp[:, 1:-1], tp[:, 2:]], axis=2)

    kw = windows(k)
    vw = windows(v)
    sc = jnp.einsum('bnqkgd,bnskd->bnkgqs', qb, kw, preferred_element_type=jnp.float32)
    off = band_offsets(blk)
    key_pos = jnp.arange(nb)[:, None, None] * blk + (jnp.arange(3 * blk) - blk)[None, None, :]
    valid = (jnp.abs(off) <= half_window)[None] & (key_pos >= 0) & (key_pos < l)
    sc = sc + bias.astype(jnp.float32).reshape(hkv, g, blk, 3 * blk)
    sc = jnp.where(valid[None, :, None, None], sc, NEG_INF)
    m = jnp.max(sc, axis=-1, keepdims=True)
    if sink is not None:
        sk = sink.astype(jnp.float32).reshape(hkv, g, 1, 1)
        m = jnp.maximum(m, sk)
    e = jnp.exp(sc - m)
    den = jnp.sum(e, axis=-1, keepdims=True)
    if sink is not None:
        den = den + jnp.exp(sk - m)
    o = jnp.einsum('bnkgqs,bnskd->bnqkgd', (e / den).astype(v.dtype), vw,
                   preferred_element_type=jnp.float32)
    lse = (m + jnp.log(den))[..., 0]
    o = o.reshape(bt, nb * blk, h, d)[:, :l].astype(q.dtype)
    lse = jnp.moveaxis(lse, -1, 2).reshape(bt, nb * blk, h)[:, :l]
    return o, lse


def hybrid_mixer(h, w_in, q_gain, k_gain, sink, w_out, rope, bias_b, bias_c):
    bt, s, _ = h.shape
    splits = np.cumsum(PROJ_SIZES)[:-1]
    qa, ka, va, qb, kb, vb, qc, kc, vc = jnp.split(h @ w_in, splits, axis=-1)

    def heads(t, n):
        return t.reshape(bt, s, n, HEAD_DIM)

    qa = apply_axial_rope(rmsnorm(heads(qa, A_HEADS), q_gain), *rope) * SCALE
    ka = apply_axial_rope(rmsnorm(heads(ka, A_KV_HEADS), k_gain), *rope)
    out_a = dense_attention_blocks(qa, ka, heads(va, A_KV_HEADS))

    out_b, _ = banded_attention(heads(qb, B_HEADS) * SCALE, heads(kb, B_KV_HEADS), heads(vb, B_KV_HEADS),
                                B_WINDOW, bias_b, sink)

    qc = heads(qc, C_HEADS) * SCALE
    kc = heads(kc, C_KV_HEADS)
    vc = heads(vc, C_KV_HEADS)
    outs = []
    lses = []
    for gi, (window, dil) in enumerate(C_PATTERNS):
        hs = slice(gi * C_HEADS_PER_GROUP, (gi + 1) * C_HEADS_PER_GROUP)
        o_g, lse_g = banded_attention(to_residue(qc[:, :, hs], dil), to_residue(kc[:, :, gi:gi + 1], dil),
                                      to_residue(vc[:, :, gi:gi + 1], dil), window // (2 * dil),
                                      bias_c[gi], None)
        outs.append(from_residue(o_g, dil, bt))
        lses.append(from_residue(lse_g, dil, bt))
    alpha = jax.nn.softmax(jnp.stack(lses, axis=2), axis=2)
    out_c = (jnp.stack(outs, axis=2).astype(jnp.float32) * alpha[..., None]).astype(h.dtype)

    mixed = jnp.concatenate([out_a.reshape(bt, s, -1), out_b.reshape(bt, s, -1),
                             out_c.reshape(bt, s, -1)], axis=-1)
    return mixed @ w_out


def memory_cross_attention(h, mem_h, w_cq, w_ckv, w_co):
    bt, s, _ = h.shape
    m = mem_h.shape[1]
    q = (h @ w_cq).reshape(bt, s, X_HEADS, HEAD_DIM) * SCALE
    kv = (mem_h @ w_ckv).reshape(bt, m, 2, X_HEADS, HEAD_DIM)
    sc = jnp.einsum('bshd,bmhd->bhsm', q, kv[:, :, 0], preferred_element_type=jnp.float32)
    p = jax.nn.softmax(sc, axis=-1).astype(kv.dtype)
    o = jnp.einsum('bhsm,bmhd->bshd', p, kv[:, :, 1], preferred_element_type=jnp.float32).astype(h.dtype)
    return o.reshape(bt, s, X_WIDTH) @ w_co


def swiglu(h, w_ffn_in, w_ffn_out):
    gate, up = jnp.split(h @ w_ffn_in, 2, axis=-1)
    return (jax.nn.silu(gate) * up) @ w_ffn_out


def encode(x, mem, ln_mix, w_in, q_norm_a, k_norm_a, sink_b, rel_bias, w_out, ln_cross, ln_mem,
           w_cq, w_ckv, w_co, ln_ffn, w_ffn_in, w_ffn_out, ln_final):
    s = x.shape[1]
    rope = axial_rope_tables(s)
    bias_b = position_bias(rel_bias, band_offsets(B_WINDOW), 0, B_HEADS)
    bias_c = [position_bias(rel_bias, band_offsets(w // (2 * d)) * d, B_HEADS + gi * C_HEADS_PER_GROUP,
                            C_HEADS_PER_GROUP) for gi, (w, d) in enumerate(C_PATTERNS)]
    for l in range(DEPTH):
        x = x + hybrid_mixer(rmsnorm(x, ln_mix[l]), w_in[l], q_norm_a[l], k_norm_a[l], sink_b[l], w_out[l],
                             rope, bias_b, bias_c)
        x = x + memory_cross_attention(rmsnorm(x, ln_cross[l]), rmsnorm(mem, ln_mem[l]),
                                       w_cq[l], w_ckv[l], w_co[l])
        x = x + swiglu(rmsnorm(x, ln_ffn[l]), w_ffn_in[l], w_ffn_out[l])
    return rmsnorm(x, ln_final)


def setup_inputs(seed: int = 0) -> dict:
    key = jax.random.key(seed)
    ks = jax.random.split(key, 24)

    def normal(k, shape, scale):
        return jax.random.normal(k, shape, jnp.float32) * scale

    def gain(k, shape):
        return 1.0 + normal(k, shape, 0.02)

    return {
        'x_prompt': normal(ks[0], (BATCH, SEQ, D_MODEL), 1.0),
        'x_sample': normal(ks[1], (DEC_BATCH, DEC_SEQ, D_MODEL), 1.0),
        'mem_prompt': normal(ks[2], (BATCH, MEM_LEN, D_MODEL), 1.0),
        'mem_sample': normal(ks[3], (DEC_BATCH, MEM_LEN, D_MODEL), 1.0),
        'ln_mix': gain(ks[4], (DEPTH, D_MODEL)),
        'w_in': normal(ks[5], (DEPTH, D_MODEL, PROJ_WIDTH), D_MODEL ** -0.5),
        'q_norm_a': gain(ks[6], (DEPTH, HEAD_DIM)),
        'k_norm_a': gain(ks[7], (DEPTH, HEAD_DIM)),
        'sink_b': normal(ks[8], (DEPTH, B_HEADS), 0.5),
        'rel_bias': normal(ks[9], (REL_BUCKETS, REL_HEADS), 0.5),
        'w_out': normal(ks[10], (DEPTH, MIX_WIDTH, D_MODEL), MIX_WIDTH ** -0.5),
        'ln_cross': gain(ks[11], (DEPTH, D_MODEL)),
        'ln_mem': gain(ks[12], (DEPTH, D_MODEL)),
        'w_cq': normal(ks[13], (DEPTH, D_MODEL, X_WIDTH), D_MODEL ** -0.5),
        'w_ckv': normal(ks[14], (DEPTH, D_MODEL, 2 * X_WIDTH), D_MODEL ** -0.5),
        'w_co': normal(ks[15], (DEPTH, X_WIDTH, D_MODEL), X_WIDTH ** -0.5),
        'ln_ffn': gain(ks[16], (DEPTH, D_MODEL)),
        'w_ffn_in': normal(ks[17], (DEPTH, D_MODEL, 2 * D_FF), D_MODEL ** -0.5),
        'w_ffn_out': normal(ks[18], (DEPTH, D_FF, D_MODEL), D_FF ** -0.5),
        'ln_final': gain(ks[19], (D_MODEL,)),
    }


def reference(x_prompt, x_sample, mem_prompt, mem_sample, ln_mix, w_in, q_norm_a, k_norm_a, sink_b, rel_bias,
              w_out, ln_cross, ln_mem, w_cq, w_ckv, w_co, ln_ffn, w_ffn_in, w_ffn_out, ln_final):
    y_prompt = encode(x_prompt, mem_prompt, ln_mix, w_in, q_norm_a, k_norm_a, sink_b, rel_bias, w_out,
                      ln_cross, ln_mem, w_cq, w_ckv, w_co, ln_ffn, w_ffn_in, w_ffn_out, ln_final)
    y_sample = encode(x_sample, mem_sample, ln_mix, w_in, q_norm_a, k_norm_a, sink_b, rel_bias, w_out,
                      ln_cross, ln_mem, w_cq, w_ckv, w_co, ln_ffn, w_ffn_in, w_ffn_out, ln_final)
    return (y_prompt, y_sample)
```

```python
import math
from contextlib import ExitStack
import numpy as np
import concourse.bass as bass
import concourse.mybir as mybir
from concourse.bass_utils import run_bass_kernel_spmd

F32 = mybir.dt.float32
BF16 = mybir.dt.bfloat16
AF = mybir.ActivationFunctionType
ALU = mybir.AluOpType

T = 8192
D = 2048
G = 512
NG = T // G
HALF = 4096
PAD = 1024
PADR = 1024 + 64
DFF = 5632
SCALE = 128 ** -0.5
EPS = 1e-6
NWS = 4


class Buf:
    __slots__ = ("w", "r")

    def __init__(self):
        self.w = None
        self.r = {}


class Tracker:
    def __init__(self):
        self.ops = {k: [] for k in ("sp", "act", "pe", "dve", "pool")}
        self.seen = {k: {} for k in self.ops}
        self.count = {"act": 0, "pe": 0, "dve": 0}
        self.semid = {"act": 0, "pe": 1, "dve": 2}
        self.nring = 20
        self.ring = {"sp": [3 + i for i in range(self.nring)], "pool": [3 + self.nring + i for i in range(self.nring)]}
        self.ringval = {}
        self.ringpos = {"sp": 0, "pool": 0}
        self.nsem = 3 + 2 * self.nring
        self.out_events = []

    def _needs(self, eng, reads, writes):
        need = {}

        def add(ev):
            if ev is None:
                return
            s, v = ev
            if need.get(s, 0) < v:
                need[s] = v

        for b in reads:
            add(b.w)
        for b in writes:
            add(b.w)
            for s, v in b.r.items():
                add((s, v))
        seen = self.seen[eng]
        wl = []
        for s, v in need.items():
            if eng == "pe" and s == self.semid["pe"]:
                continue
            if seen.get(s, 0) >= v:
                continue
            seen[s] = v
            wl.append((s, v))
        return wl

    def _post(self, ev, reads, writes):
        s, v = ev
        for b in reads:
            if b.r.get(s, 0) < v:
                b.r[s] = v
        for b in writes:
            b.w = ev
            b.r = {}

    def op(self, eng, fn, reads=(), writes=()):
        wl = self._needs(eng, reads, writes)
        self.count[eng] += 1
        ev = (self.semid[eng], self.count[eng])
        self.ops[eng].append((wl, fn, ev))
        self._post(ev, reads, writes)
        return ev

    def dma(self, q, pairs, reads=(), writes=(), is_output=False):
        i = self.ringpos[q]
        self.ringpos[q] = (i + 1) % self.nring
        s = self.ring[q][i]
        prev = self.ringval.get(s, 0)
        wl = self._needs(q, reads, writes)
        seen = self.seen[q]
        if prev > 0 and seen.get(s, 0) < prev:
            seen[s] = prev
            wl.append((s, prev))
        newv = prev + 16 * len(pairs)
        self.ringval[s] = newv
        ev = (s, newv)
        self.ops[q].append((wl, pairs, ev))
        self._post(ev, reads, writes)
        if is_output:
            self.out_events.append(ev)
        return ev

    def replay(self, eng, e, sems):
        for wl, fn, ev in self.ops[eng]:
            for s, v in wl:
                e.wait_ge(sems[s], v)
            if eng in ("sp", "pool"):
                for o, i_ in fn:
                    e.dma_start(out=o, in_=i_).then_inc(sems[ev[0]], 16)
            else:
                ins = None
                for name, kw in fn:
                    ins = getattr(e, name)(**kw)
                ins.then_inc(sems[ev[0]], 1)


def I(name, **kw):
    return (name, kw)


def MM(out, lhsT, rhs, start, stop):
    return ("matmul", dict(out=out, lhsT=lhsT, rhs=rhs, start=start, stop=stop))


class Arena:
    def __init__(self):
        self.live = []
        self.merged = {}

    def begin(self):
        m = dict(self.merged)
        for b in self.live:
            if b.w is not None:
                s_, v_ = b.w
                if m.get(s_, 0) < v_:
                    m[s_] = v_
            for s_, v_ in b.r.items():
                if m.get(s_, 0) < v_:
                    m[s_] = v_
        self.merged = m
        self.live = []

    def buf(self):
        b = Buf()
        b.r = dict(self.merged)
        self.live.append(b)
        return b


def build(nlayers=4, dbg=False):
    nc = bass.Bass("TRN2", target_bir_lowering=False)
    tr = Tracker()
    arena = Arena()

    def din(name, shape, dt=F32):
        return nc.dram_tensor(name, list(shape), dt, kind="ExternalInput").ap()

    def dscr(name, shape, dt, force_internal=False):
        kind = "ExternalOutput" if (dbg and not force_internal) else "Internal"
        return nc.dram_tensor(name, list(shape), dt, kind=kind).ap()

    L = 4
    x_in = din("x", [T, D])
    mem_in = din("mem", [512, D])
    ln_mix = din("ln_mix", [L, D]); w_in = din("w_in", [L, D, 3840])
    q_norm_a = din("q_norm_a", [L, 128]); k_norm_a = din("k_norm_a", [L, 128])
    sink_b = din("sink_b", [1, 16]); rel_bias = din("rel_bias", [32, 10])
    w_out = din("w_out", [L, D, D]); ln_cross = din("ln_cross", [L, D]); ln_mem = din("ln_mem", [L, D])
    w_cq = din("w_cq", [L, D, 512]); w_ckv = din("w_ckv", [L, D, 1024]); w_co = din("w_co", [L, 512, D])
    ln_ffn = din("ln_ffn", [L, D]); w_ffn_in = din("w_ffn_in", [L, D, 2 * DFF]); w_ffn_out = din("w_ffn_out", [L, DFF, D])
    ln_final = din("ln_final", [1, D])
    rope_in = din("rope", [T, 256])
    oh_in = din("oh", [32, 3 * 512])
    mk_in = din("mk", [10, 512])
    flags_in = din("flags", [128, 8])
    ident_in = din("ident", [128, 128])
    jmat_in = din("jmat", [128, 128])
    y_out = nc.dram_tensor("y", [T, D], F32, kind="ExternalOutput").ap()

    xres = dscr("xres", [T, D], F32)
    qaT = dscr("qaT", [6, 128, T], BF16); kaT = dscr("kaT", [2, 128, T], BF16); va = dscr("va", [T, 256], BF16)
    qbT = dscr("qbT", [4, 128, T], BF16); kbT = dscr("kbT", [2, 128, PAD + T + PAD], BF16)
    vb = dscr("vb", [PAD + T + PADR, 256], BF16)
    qcT = dscr("qcT", [6, 128, T], BF16); kcT = dscr("kcT", [3, 128, PAD + T + PAD], BF16)
    vc = dscr("vc", [PAD + T + PADR, 384], BF16)
    mixbc = dscr("mixbc", [10, 128, T], BF16)
    evec = dscr("evec", [10, 512], F32)
    wsc = []
    for l in range(nlayers):
        d_ = {}
        d_["in_fm"] = dscr(f"s_in_fm{l}", [8, 128, 16, 256], BF16, True)
        d_["cq_fm"] = dscr(f"s_cq_fm{l}", [2, 128, 16, 256], BF16, True)
        d_["ckv_fm"] = dscr(f"s_ckv_fm{l}", [2, 128, 16, 256], BF16, True)
        d_["ffi_fm"] = dscr(f"s_ffi_fm{l}", [44, 128, 16, 256], BF16, True)
        d_["in_tm"] = dscr(f"s_in_tm{l}", [4, 2, 128, 8, 512], BF16, True)
        d_["out_tm"] = dscr(f"s_out_tm{l}", [4, 2, 128, 8, 512], BF16, True)
        d_["ckv_tm"] = dscr(f"s_ckv_tm{l}", [1, 2, 128, 8, 512], BF16, True)
        d_["co_tm"] = dscr(f"s_co_tm{l}", [4, 1, 128, 8, 512], BF16, True)
        d_["ffo_tm"] = dscr(f"s_ffo_tm{l}", [4, 6, 128, 8, 512], BF16, True)
        wsc.append(d_)
    wbuf = [{k: Buf() for k in wsc[l]} for l in range(nlayers)]

    st = ExitStack()

    def sb(name, shape, dt):
        return st.enter_context(nc.sbuf_tensor(name, list(shape), dt))

    WS = sb("WS", [128, NWS, 4096], BF16)
    XT = sb("XT", [128, 4, 2048], F32)
    HT = sb("HT", [128, 16, 512], BF16)
    BIG = sb("BIG", [128, 32768], BF16)
    MISC = sb("MISC", [128, 5120], BF16)
    TMPF = sb("TMPF", [128, 4, 512], F32)
    GA = sb("GA", [128, 2048], F32)
    GB = sb("GB", [128, 2048], F32)
    EB = sb("EB", [128, 10, 384], F32)
    ROPE = sb("ROPE", [128, 8, 256], F32)
    G01 = sb("G01", [128, 2, 512], F32)
    IDB = sb("IDB", [128, 128], BF16)
    ONESB = sb("ONESB", [128, 128], BF16)
    JF = sb("JF", [128, 128], F32)
    STAT = sb("STAT", [128, 64], F32)
    ESINK = sb("ESINK", [128, 16], F32)
    FLAGS = sb("FLAGS", [128, 8], F32)
    SETUP = TMPF[0:32, :, :].rearrange("p a b -> p (a b)")[:, 0:10 + 3 * 512]
    E3 = GB[0:10, :].rearrange("p (a b) -> p a b", a=4)
    PSP = [st.enter_context(nc.psum_tensor(f"PSP{i}", [128, 1024], F32)) for i in range(3)]
    PS = [PSP[i // 2][:, (i % 2) * 512:(i % 2 + 1) * 512] for i in range(6)]
    PTB = [st.enter_context(nc.psum_tensor(f"PTB{i}", [128, 1024], BF16)) for i in range(2)]
    PTBF = [PTB[i].bitcast(F32) for i in range(2)]
    sems = [st.enter_context(nc.semaphore(f"s{i}")) for i in range(tr.nsem)]

    WSb = [Buf() for _ in range(NWS)]
    XTb = [Buf() for _ in range(4)]
    HTb = [Buf() for _ in range(4)]
    PSb = [Buf() for _ in range(6)]
    PTBb = [Buf() for _ in range(2)]
    TMPFb = [Buf() for _ in range(4)]
    GAb, GBb, EBb, G01b, CONSTb, ESINKb = Buf(), Buf(), Buf(), Buf(), Buf(), Buf()
    ROPEb = [Buf() for _ in range(8)]
    STATb = [Buf() for _ in range(8)]
    xb = [Buf() for _ in range(NG)]
    p1b = [Buf() for _ in range(NG)]
    padb = Buf()
    mixb = [Buf() for _ in range(4)]

    state = {"ws": 0, "stat": 0, "cp": 0}

    def act(ins, reads, writes):
        return tr.op("act", ins, reads, writes)

    def dve(ins, reads, writes):
        return tr.op("dve", ins, reads, writes)

    def pe(ins, reads, writes):
        return tr.op("pe", ins, reads, writes)

    def copy_any(out, in_, reads, writes):
        state["cp"] += 1
        if state["cp"] % 2:
            return act([I("activation", out=out, in_=in_, func=AF.Copy)], reads, writes)
        return dve([I("tensor_copy", out=out, in_=in_)], reads, writes)

    def wload(src_ap, wb, shape3):
        s = state["ws"]
        state["ws"] = (s + 1) % NWS
        n = shape3[0] * shape3[1]
        view = WS[:, s, 0:n].rearrange("p (a b) -> p a b", a=shape3[0])
        tr.dma("sp", [(view, src_ap)], reads=[wb], writes=[WSb[s]])
        return view, WSb[s]

    def stat_slot():
        s = state["stat"]
        state["stat"] = (s + 1) % 8
        return STAT[:, s * 8:(s + 1) * 8], STATb[s]

    def load_gain(tile, tb, src_row):
        tr.dma("sp", [(tile[:, :], src_row.to_broadcast([128, D]))], reads=[], writes=[tb])

    def norm_stage_a(xaps, xbufs, gain, gainb, hbs, hbbs):
        n = len(xaps)
        sv, sbuf_ = stat_slot()
        for t in range(n):
            act([I("activation", out=hbs[t], in_=xaps[t], func=AF.Square, accum_out=sv[:, t:t + 1])], [xbufs[t]], [hbbs[t], sbuf_])
        act([I("activation", out=sv[:, 4:4 + n], in_=sv[:, 0:n], func=AF.Sqrt, bias=EPS, scale=1.0 / D)], [sbuf_], [sbuf_])
        dve([I("reciprocal", out=sv[:, 4:4 + n], in_=sv[:, 4:4 + n])], [sbuf_], [sbuf_])
        for t in range(n):
            dve([I("scalar_tensor_tensor", out=hbs[t], in0=xaps[t], scalar=sv[:, 4 + t:5 + t], in1=gain[:, :], op0=ALU.mult, op1=ALU.mult)],
                [xbufs[t], sbuf_, gainb], [hbbs[t]])

    def norm_stage_b(n, hbs, hbbs, htdst, htbufs):
        for t in range(n):
            for k in range(2):
                ins = [I("transpose", out=PTB[k][:, j * 128:(j + 1) * 128], in_=hbs[t][:, (k * 8 + j) * 128:(k * 8 + j + 1) * 128], identity=IDB[:, :])
                       for j in range(8)]
                pe(ins, [hbbs[t], CONSTb], [PTBb[k]])
                copy_any(htdst[:, k * 8:(k + 1) * 8, t * 128:(t + 1) * 128], PTB[k][:, :].rearrange("p (j q) -> p j q", j=8),
                         [PTBb[k]], [htbufs[t]])

    def fm_matmul(psum_ap, psb, wview, wb, joff, rhs_fn, rhs_bufs, kc=16):
        ins = [MM(psum_ap, wview[:, c, joff:joff + 128], rhs_fn(c), c == 0, c == kc - 1) for c in range(kc)]
        pe(ins, [wb] + list(rhs_bufs), [psb])

    def tm_matmul(psum_ap, psb, lhs_fn, lhs_bufs, wview, wb, ncols, c0, nch, ktot):
        ins = [MM(psum_ap, lhs_fn(c0 + cc), wview[:, cc, 0:ncols], (c0 + cc) == 0, (c0 + cc) == ktot - 1) for cc in range(nch)]
        pe(ins, [wb] + list(lhs_bufs), [psb])

    def conv_ops(l):
        ops = []
        S = wsc[l]

        def fm_range(dst, b0, nb, j0, jw, src, col0):
            prs = []
            for c in range(16):
                o = dst[b0:b0 + nb, :, c, j0:j0 + jw].rearrange("b p j -> p b j")
                i_ = src[c * 128:(c + 1) * 128, col0:col0 + nb * jw].rearrange("p (b j) -> p b j", j=jw)
                prs.append((o, i_))
            return prs

        def tm_range(dst, nb0, nnb, j0, jw, src, col0, kch):
            prs = []
            for c in range(kch):
                o = dst[nb0:nb0 + nnb, c // 8, :, c % 8, j0:j0 + jw].rearrange("b p j -> p b j")
                i_ = src[c * 128:(c + 1) * 128, col0:col0 + nnb * jw].rearrange("p (b j) -> p b j", j=jw)
                prs.append((o, i_))
            return prs

        wi = w_in[l]
        ops.append(("in_tm", tm_range(S["in_tm"], 0, 2, 0, 512, wi, 0, 16)
                    + tm_range(S["in_tm"], 2, 1, 0, 256, wi, 1024, 16)
                    + tm_range(S["in_tm"], 2, 1, 256, 256, wi, 2048, 16)
                    + tm_range(S["in_tm"], 3, 1, 0, 384, wi, 3456, 16)))
        ops.append(("in_fm", fm_range(S["in_fm"], 0, 3, 0, 256, wi, 1280)
                    + fm_range(S["in_fm"], 3, 4, 0, 256, wi, 2304)
                    + fm_range(S["in_fm"], 7, 1, 0, 128, wi, 3328)))
        ops.append(("out_tm", tm_range(S["out_tm"], 0, 4, 0, 512, w_out[l], 0, 16)))
        ops.append(("ckv_fm", fm_range(S["ckv_fm"], 0, 2, 0, 256, w_ckv[l], 0)))
        ops.append(("ckv_tm", tm_range(S["ckv_tm"], 0, 1, 0, 512, w_ckv[l], 512, 16)))
        ops.append(("cq_fm", fm_range(S["cq_fm"], 0, 2, 0, 256, w_cq[l], 0)))
        ops.append(("co_tm", tm_range(S["co_tm"], 0, 4, 0, 512, w_co[l], 0, 4)))
        ops.append(("ffi_fm", fm_range(S["ffi_fm"], 0, 44, 0, 128, w_ffn_in[l], 0)
                    + fm_range(S["ffi_fm"], 0, 44, 128, 128, w_ffn_in[l], DFF)))
        ops.append(("ffo_tm", tm_range(S["ffo_tm"], 0, 4, 0, 512, w_ffn_out[l], 0, 44)))
        return ops

    def emit_conv(l, key, prs):
        for i in range(0, len(prs), 16):
            tr.dma("pool", prs[i:i + 16], reads=[], writes=[wbuf[l][key]])

    def setup():
        tr.dma("pool", [(IDB[:, :], ident_in[:, :])], [], [CONSTb])
        tr.dma("sp", [(JF[:, :], jmat_in[:, :]), (FLAGS[:, :], flags_in[:, :]),
                      (SETUP[:, 0:10], rel_bias[:, :]), (SETUP[:, 10:10 + 1536], oh_in[:, :]),
                      (E3[:, 3, :], mk_in[:, :]),
                      (ESINK[:, :], sink_b[0:1, :].to_broadcast([128, 16]))], [], [CONSTb, ESINKb, GBb] + TMPFb)
        dve([I("memset", ap=ONESB[:, :], constant=1.0)], [], [CONSTb])
        act([I("activation", out=ESINK[:, :], in_=ESINK[:, :], func=AF.Exp)], [ESINKb], [ESINKb])
        zt = XT.bitcast(BF16)
        ztz = zt[:, 0, :] if len(zt.shape) == 3 else zt[:, 0:4096]
        dve([I("memset", ap=XT[:, 0, :], constant=0.0)], [], [XTb[0]])
        prs = []
        for h in range(2):
            prs.append((kbT[h, :, 0:PAD], ztz[:, 0:PAD]))
            prs.append((kbT[h, :, PAD + T:PAD + T + PAD], ztz[:, 0:PAD]))
        for h in range(3):
            prs.append((kcT[h, :, 0:PAD], ztz[:, 0:PAD]))
            prs.append((kcT[h, :, PAD + T:PAD + T + PAD], ztz[:, 0:PAD]))
        for (v_, w_) in ((vb, 256), (vc, 384)):
            for r0 in range(0, PAD, 128):
                prs.append((v_[r0:r0 + 128, :], ztz[:, 0:w_]))
            for r0 in range(0, PADR, 128):
                n = min(128, PADR - r0)
                prs.append((v_[PAD + T + r0:PAD + T + r0 + n, :], ztz[0:n, 0:w_]))
        tr.dma("pool", prs, [XTb[0]], [padb])
        for gt in range(3):
            pe([MM(PS[gt][0:10, :], SETUP[:, 0:10], SETUP[:, 10 + gt * 512:10 + (gt + 1) * 512], True, True)], [CONSTb] + TMPFb, [PSb[gt]])
            act([I("activation", out=E3[:, gt, :], in_=PS[gt][0:10, :], func=AF.Exp)], [PSb[gt]], [EBb, GBb])
            dve([I("tensor_tensor", out=E3[:, gt, :], in0=E3[:, gt, :], in1=E3[:, 3, :], op=ALU.mult)], [EBb, CONSTb, GBb], [EBb, GBb])
        evb = Buf()
        tr.dma("pool", [(evec[0:6, :], E3[0:6, 0, :]), (evec[6:8, :], E3[6:8, 1, :]), (evec[8:10, :], E3[8:10, 2, :])], [EBb, GBb], [evb])
        hk = XT[:, 1, :]
        for h in range(10):
            offs = (-128, 0, 128) if h < 4 else (-64, 64)
            prs = []
            for a, off in enumerate(offs):
                base = 129 - off
                src = bass.AP(tensor=evec.tensor, offset=h * 512 + base, ap=[[1, 128], [1, 128]])
                prs.append((hk[:, a * 128:(a + 1) * 128], src))
            tr.dma("sp", prs, [evb], [XTb[1]])
            n = len(offs) * 128
            bk = 3 + h % 2
            pe([MM(PS[bk][:, 0:n], JF[:, :], hk[:, 0:n], True, True)], [XTb[1], CONSTb], [PSb[bk]])
            dve([I("tensor_copy", out=EB[:, h, 0:n], in_=PS[bk][:, 0:n])], [PSb[bk]], [EBb])

    def phase1(l):
        arena.begin()
        xsrc = x_in if l == 0 else xres
        load_gain(GA, GAb, ln_mix[l:l + 1, :])
        tr.dma("sp", [(G01[:, 0, :].rearrange("p (h d) -> p h d", h=4), q_norm_a[l:l + 1, :].unsqueeze(1).to_broadcast([128, 4, 128])),
                      (G01[:, 1, 0:256].rearrange("p (h d) -> p h d", h=2), q_norm_a[l:l + 1, :].unsqueeze(1).to_broadcast([128, 2, 128])),
                      (G01[:, 1, 256:512].rearrange("p (h d) -> p h d", h=2), k_norm_a[l:l + 1, :].unsqueeze(1).to_broadcast([128, 2, 128]))],
               [], [G01b])
        HB = [BIG[:, i * 2048:(i + 1) * 2048] for i in range(4)]
        HBb = [arena.buf() for _ in range(4)]
        QKST = BIG[:, 8192:12288].rearrange("p (h t) -> p h t", h=8)
        VST = BIG[:, 12288:12288 + 3584].rearrange("p (t c) -> p t c", t=4)
        FMST = BIG[:, 15872:15872 + 7680].rearrange("p (h t) -> p h t", h=15)
        OBs = [MISC[:, i * 512:(i + 1) * 512].rearrange("p (h d) -> p h d", h=4) for i in range(4)]
        obbs = [arena.buf() for _ in range(4)]
        HT2 = BIG[:, 24576:32768].rearrange("p (c t) -> p c t", c=16)
        HT2b = [arena.buf() for _ in range(4)]
        HTs = [(HT, HTb), (HT2, HT2b)]
        qkstb, vstb, fmstb = arena.buf(), arena.buf(), arena.buf()
        S = wsc[l]

        def qk_transposes(nb, t):
            pe([I("transpose", out=PTB[nb][:, h * 128:(h + 1) * 128], in_=OBs[t][:, h, :], identity=IDB[:, :]) for h in range(4)],
               [obbs[t], CONSTb], [PTBb[nb]])
            copy_any(QKST[:, nb * 4:(nb + 1) * 4, t * 128:(t + 1) * 128],
                     PTB[nb][:, 0:512].rearrange("p (h q) -> p h q", h=4), [PTBb[nb]], [qkstb])

        def p1_stage_a(g):
            for t in range(4):
                tt = 4 * g + t
                tr.dma("sp", [(XT[:, t, :], xsrc[tt * 128:(tt + 1) * 128, :])], [xb[g]], [XTb[t]])
            rs = (g % 2) * 4
            tr.dma("sp", [(ROPE[:, rs + t, :], rope_in[(4 * g + t) * 128:(4 * g + t + 1) * 128, :]) for t in range(4)], [],
                   [ROPEb[rs + t] for t in range(4)])
            norm_stage_a([XT[:, t, :] for t in range(4)], XTb, GA, GAb, HB, HBb)

        def p1_stage_b(g):
            htd, htdb = HTs[g % 2]
            norm_stage_b(4, HB, HBb, htd, htdb)

        p1_stage_a(0)
        p1_stage_b(0)
        for g in range(NG):
            g0 = g * G
            HTg, HTgb = HTs[g % 2]
            if g + 1 < NG:
                p1_stage_a(g + 1)
            deferred = []
            for nb in range(4):
                ncols = 384 if nb == 3 else 512
                for seg in range(2):
                    wv, wb_ = wload(S["in_tm"][nb, seg], wbuf[l]["in_tm"], (8, 512))
                    for t in range(4):
                        bk = (4 * nb + t) % 6
                        tm_matmul(PS[bk][:, 0:ncols], PSb[bk], lambda c, t=t: HTg[:, c, t * 128:(t + 1) * 128], [HTgb[t]],
                                  wv, wb_, ncols, seg * 8, 8, 16)
                for (dnb, dt) in deferred:
                    qk_transposes(dnb, dt)
                deferred = []
                for t in range(4):
                    bk = (4 * nb + t) % 6
                    if nb >= 2:
                        dst = VST[:, t, 0:512] if nb == 2 else VST[:, t, 512:896]
                        copy_any(dst, PS[bk][:, 0:ncols], [PSb[bk]], [vstb])
                        continue
                    rb = ROPEb[(g % 2) * 4 + t]
                    RC = ROPE[:, (g % 2) * 4 + t, 0:128]
                    RS = ROPE[:, (g % 2) * 4 + t, 128:256]
                    psv = PS[bk][:, :].rearrange("p (h d) -> p h d", h=4)
                    sv, sbuf_ = stat_slot()
                    junk = TMPF[:, 3, 0:128]
                    for h in range(4):
                        act([I("activation", out=junk, in_=psv[:, h, :], func=AF.Square, accum_out=sv[:, h:h + 1])],
                            [PSb[bk]], [TMPFb[3], sbuf_])
                    act([I("activation", out=sv[:, 4:8], in_=sv[:, 0:4], func=AF.Sqrt, bias=EPS, scale=1.0 / 128)], [sbuf_], [sbuf_])
                    dve([I("reciprocal", out=sv[:, 4:8], in_=sv[:, 4:8])], [sbuf_], [sbuf_])
                    QN = TMPF[:, 0, :].rearrange("p (h d) -> p h d", h=4)
                    T1 = TMPF[:, 1, :].rearrange("p (h d) -> p h d", h=4)
                    T2 = TMPF[:, 2, :].rearrange("p (h d) -> p h d", h=4)
                    for h in range(4):
                        dve([I("scalar_tensor_tensor", out=QN[:, h, :], in0=psv[:, h, :], scalar=sv[:, 4 + h:5 + h],
                               in1=G01[:, nb, h * 128:(h + 1) * 128], op0=ALU.mult, op1=ALU.mult)],
                            [PSb[bk], sbuf_, G01b], [TMPFb[0]])
                    dve([I("tensor_tensor", out=T1, in0=QN, in1=RC.unsqueeze(1).to_broadcast([128, 4, 128]), op=ALU.mult)],
                        [TMPFb[0], rb], [TMPFb[1]])
                    QN5 = TMPF[:, 0, :].rearrange("p (h a b f) -> p h a b f", h=4, a=2, b=2)
                    T25 = TMPF[:, 2, :].rearrange("p (h a b f) -> p h a b f", h=4, a=2, b=2)
                    RS4 = RS.rearrange("p (a b f) -> p a b f", a=2, b=2)
                    for h in range(4):
                        dve([I("tensor_tensor", out=T25[:, h, :, 0, :], in0=QN5[:, h, :, 1, :], in1=RS4[:, :, 0, :], op=ALU.mult)],
                            [TMPFb[0], rb], [TMPFb[2]])
                        dve([I("tensor_tensor", out=T25[:, h, :, 1, :], in0=QN5[:, h, :, 0, :], in1=RS4[:, :, 1, :], op=ALU.mult)],
                            [TMPFb[0], rb], [TMPFb[2]])
                    dve([I("tensor_tensor", out=OBs[t], in0=T1, in1=T2, op=ALU.add)], [TMPFb[1], TMPFb[2]], [obbs[t]])
                    deferred.append((nb, t))
            tr.dma("pool", [(qaT[:, :, g0:g0 + G].rearrange("h p t -> p h t"), QKST[:, 0:6, :]),
                            (kaT[:, :, g0:g0 + G].rearrange("h p t -> p h t"), QKST[:, 6:8, :])], [qkstb], [p1b[g]])
            tr.dma("pool", [(va[g0:g0 + G, :].rearrange("(t p) c -> p t c", p=128), VST[:, :, 0:256]),
                            (vb[PAD + g0:PAD + g0 + G, :].rearrange("(t p) c -> p t c", p=128), VST[:, :, 256:512]),
                            (vc[PAD + g0:PAD + g0 + G, :].rearrange("(t p) c -> p t c", p=128), VST[:, :, 512:896])], [vstb], [p1b[g]])
            if g + 1 < NG:
                p1_stage_b(g + 1)
            for b in range(8):
                wv, wb_ = wload(S["in_fm"][b], wbuf[l]["in_fm"], (16, 256))
                for jj in range(2):
                    fh = 2 * b + jj
                    if fh >= 15:
                        continue
                    bk = fh % 6
                    fm_matmul(PS[bk][:, :], PSb[bk], wv, wb_, jj * 128, lambda c: HTg[:, c, :], HTgb)
                    copy_any(FMST[:, fh, :], PS[bk][:, :], [PSb[bk]], [fmstb])
            tr.dma("pool", [(qbT[:, :, g0:g0 + G].rearrange("h p t -> p h t"), FMST[:, 0:4, :]),
                            (kbT[:, :, PAD + g0:PAD + g0 + G].rearrange("h p t -> p h t"), FMST[:, 4:6, :]),
                            (qcT[:, :, g0:g0 + G].rearrange("h p t -> p h t"), FMST[:, 6:12, :]),
                            (kcT[:, :, PAD + g0:PAD + g0 + G].rearrange("h p t -> p h t"), FMST[:, 12:15, :])], [fmstb], [p1b[g]])

    def phase2a(l):
        arena.begin()
        QT = [BIG[:, 0:2048], BIG[:, 2048:4096]]
        KT = [BIG[:, 4096:8192], BIG[:, 8192:12288]]
        VT = [BIG[:, 12288:16384].rearrange("p (m d) -> p m d", m=32), BIG[:, 16384:20480].rearrange("p (m d) -> p m d", m=32)]
        OST = [BIG[:, 20480:22528], BIG[:, 22528:24576]]
        qtb = [arena.buf(), arena.buf()]
        ktb = [arena.buf(), arena.buf()]
        vtb = [arena.buf(), arena.buf()]
        ostb = [arena.buf(), arena.buf()]
        PTm = [MISC[:, i * 512:(i + 1) * 512] for i in range(4)]
        PTb = [arena.buf() for _ in range(4)]
        cnt = {"q": 0, "kv": 0, "o": 0, "qb": 0, "chunk": 0}

        def run_head(s, qsrc, d, offs, ebh, sink_col, out_mode, gidx, ki):
            p0 = s * 2048
            nt = len(offs)
            off0 = offs[0]
            qi = cnt["q"] % 2
            cnt["q"] += 1
            rd = [p1b[g] for g in range(4 * s, 4 * s + 4)]
            tr.dma("sp", [(QT[qi], qsrc[:, p0:p0 + 2048])], rd, [qtb[qi]])
            QTv = QT[qi].rearrange("p (i d) -> p d i", d=d)
            H = -off0 * d
            KTv = KT[ki][:, 0:2048 + 2 * H].rearrange("p (i d) -> p d i", d=d)
            nqb = 2048 // (128 * d)
            ntm = nqb + nt - 1
            qblocks = [(r, b) for r in range(d) for b in range(nqb)]
            oi = None
            if out_mode == "B":
                oi = cnt["o"] % 2
                cnt["o"] += 1
            descs = []
            for ci in range(4):
                ck = cnt["chunk"]
                cnt["chunk"] += 1
                bo, bd = 2 + 2 * (ck % 2), 3 + 2 * (ck % 2)
                for qq in range(4):
                    r, b = qblocks[ci * 4 + qq]
                    n = cnt["qb"]
                    cnt["qb"] += 1
                    alist = []
                    for a, off in enumerate(offs):
                        lo = p0 + r + d * (128 * b + off)
                        hi = lo + d * 127
                        if hi < 0 or lo >= T:
                            continue
                        alist.append(a)
                    descs.append(dict(ci=ci, ck=ck, bo=bo, bd=bd, qq=qq, r=r, b=b, n=n, bs=n % 2, alist=alist))

            def emit_qk(dc):
                r, b, bs = dc["r"], dc["b"], dc["bs"]
                ins = []
                for a in dc["alist"]:
                    i0 = 128 * b + offs[a] - off0
                    ins.append(MM(PS[bs][:, a * 128:(a + 1) * 128], KTv[:, r, i0:i0 + 128], QTv[:, r, b * 128:(b + 1) * 128], True, True))
                pe(ins, [ktb[ki], qtb[qi]], [PSb[bs]])

            def emit_rest(dc):
                r, b, bs, n, qq, bo, bd, alist = dc["r"], dc["b"], dc["bs"], dc["n"], dc["qq"], dc["bo"], dc["bd"], dc["alist"]
                a_lo, a_hi = alist[0], alist[-1] + 1
                tf = TMPF[:, n % 2, a_lo * 128:a_hi * 128]
                tfb = TMPFb[n % 2]
                act([I("activation", out=tf, in_=PS[bs][:, a_lo * 128:a_hi * 128], func=AF.Exp, scale=SCALE)], [PSb[bs]], [tfb])
                pt = PTm[n % 4]
                ptb = PTb[n % 4]
                dve([I("tensor_tensor", out=pt[:, a_lo * 128:a_hi * 128], in0=tf, in1=EB[:, ebh, a_lo * 128:a_hi * 128], op=ALU.mult)],
                    [tfb, EBb], [ptb])
                for a in alist:
                    lo = p0 + r + d * (128 * b + offs[a])
                    hi = lo + d * 127
                    col = None
                    if lo < 0:
                        col = 2
                    elif hi >= T:
                        col = 3
                    elif p0 >= HALF and lo < HALF:
                        col = 0 if hi < HALF else 4
                    elif p0 < HALF and hi >= HALF:
                        col = 0 if lo >= HALF else 5
                    if col is not None:
                        dve([I("tensor_scalar", out=pt[:, a * 128:(a + 1) * 128], in0=pt[:, a * 128:(a + 1) * 128],
                               scalar1=FLAGS[:, col:col + 1], scalar2=None, op0=ALU.mult)], [ptb, CONSTb], [ptb])
                mbase = r * ntm + b
                ins = []
                for a in alist:
                    ins.append(MM(PS[bo][:, qq * 128:(qq + 1) * 128], VT[ki][:, mbase + a, :], pt[:, a * 128:(a + 1) * 128],
                                  a == alist[0], a == alist[-1]))
                for a in alist:
                    ins.append(MM(PS[bd][:, qq * 128:(qq + 1) * 128], ONESB[:, :], pt[:, a * 128:(a + 1) * 128],
                                  a == alist[0], a == alist[-1]))
                pe(ins, [ptb, vtb[ki], CONSTb], [PSb[bo], PSb[bd]])

            def emit_evac(ci, ck, bo, bd):
                if d == 1:
                    dsel = lambda buf: buf[:, ci * 512:(ci + 1) * 512]
                    ssel = lambda p_: p_[:, :]
                elif d == 4:
                    dsel = lambda buf: buf.rearrange("p (i d) -> p d i", d=4)[:, ci, :]
                    ssel = lambda p_: p_[:, :]
                else:
                    dsel = lambda buf: buf.rearrange("p (i d) -> p d i", d=16)[:, 4 * ci:4 * ci + 4, :]
                    ssel = lambda p_: p_[:, :].rearrange("p (r i) -> p r i", r=4)
                if out_mode == "B":
                    R = TMPF[:, 2 + ck % 2, :]
                    Rb = TMPFb[2 + ck % 2]
                    dve([I("tensor_scalar", out=R, in0=PS[bd][:, :], scalar1=ESINK[:, sink_col:sink_col + 1], scalar2=None, op0=ALU.add)],
                        [PSb[bd], ESINKb], [Rb])
                    dve([I("reciprocal", out=R, in_=R)], [Rb], [Rb])
                    dve([I("tensor_tensor", out=dsel(OST[oi]), in0=ssel(PS[bo]), in1=R, op=ALU.mult)], [PSb[bo], Rb], [ostb[oi]])
                else:
                    Obuf = XT[:, gidx, :]
                    Dsum = XT[:, 3, :]
                    act([I("activation", out=dsel(Obuf), in_=ssel(PS[bo]), func=AF.Copy)], [PSb[bo]], [XTb[gidx]])
                    if gidx == 0:
                        dve([I("tensor_copy", out=dsel(Dsum), in_=ssel(PS[bd]))], [PSb[bd]], [XTb[3]])
                    else:
                        dve([I("tensor_tensor", out=dsel(Dsum), in0=ssel(PS[bd]), in1=dsel(Dsum), op=ALU.add)], [PSb[bd], XTb[3]], [XTb[3]])

            emit_qk(descs[0])
            for i_, dc in enumerate(descs):
                if i_ + 1 < len(descs):
                    emit_qk(descs[i_ + 1])
                emit_rest(dc)
                if dc["qq"] == 3:
                    emit_evac(dc["ci"], dc["ck"], dc["bo"], dc["bd"])
            return oi

        def load_kv(s, ksrc, vsrc, vcol0, d, offs):
            p0 = s * 2048
            nt = len(offs)
            off0 = offs[0]
            H = -off0 * d
            ki = cnt["kv"] % 2
            cnt["kv"] += 1
            glo = max(0, (p0 - H) // G)
            ghi = min(NG - 1, (p0 + 2048 + H - 1) // G)
            rd = [p1b[g] for g in range(glo, ghi + 1)] + [padb]
            tr.dma("sp", [(KT[ki][:, 0:2048 + 2 * H], ksrc[:, PAD + p0 - H:PAD + p0 + 2048 + H])], rd, [ktb[ki]])
            nqb = 2048 // (128 * d)
            ntm = nqb + nt - 1
            prs = []
            for r in range(d):
                row0 = PAD + p0 + r + d * off0
                rows = vsrc[row0:row0 + ntm * 128 * d, vcol0:vcol0 + 128]
                src = rows.rearrange("(m k dd) c -> dd k m c", k=128, dd=d)[0]
                prs.append((VT[ki][:, r * ntm:(r + 1) * ntm, :], src))
            tr.dma("sp", prs, rd, [vtb[ki]])
            return ki

        for s in range(4):
            p0 = s * 2048
            for kv in range(2):
                ki = load_kv(s, kbT[kv], vb, kv * 128, 1, (-128, 0, 128))
                for hh in range(2):
                    h = 2 * kv + hh
                    oi = run_head(s, qbT[h], 1, (-128, 0, 128), h, l * 4 + h, "B", 0, ki)
                    tr.dma("pool", [(mixbc[h, :, p0:p0 + 2048], OST[oi])], [ostb[oi]], [mixb[s]])
            for j in range(2):
                for gi, d in enumerate((1, 4, 16)):
                    ki = load_kv(s, kcT[gi], vc, gi * 128, d, (-64, 64))
                    hc = 2 * gi + j
                    run_head(s, qcT[hc], d, (-64, 64), 4 + hc, None, "C", gi, ki)
                Dsum = XT[:, 3, :]
                dve([I("reciprocal", out=Dsum, in_=Dsum)], [XTb[3]], [XTb[3]])
                for gi in range(3):
                    oi = cnt["o"] % 2
                    cnt["o"] += 1
                    dve([I("tensor_tensor", out=OST[oi], in0=XT[:, gi, :], in1=Dsum, op=ALU.mult)], [XTb[gi], XTb[3]], [ostb[oi]])
                    tr.dma("pool", [(mixbc[4 + 2 * gi + j, :, p0:p0 + 2048], OST[oi])], [ostb[oi]], [mixb[s]])

    def phase2b(l):
        arena.begin()
        xsrc = x_in if l == 0 else xres
        KAT = BIG[:, 0:16384].rearrange("p (h t) -> p h t", h=2)
        VA = BIG[:, 16384:32768].rearrange("p (m c) -> p m c", m=64)
        katb, vab = arena.buf(), arena.buf()
        QA = MISC[:, 0:3072].rearrange("p (h t) -> p h t", h=6)
        qab = arena.buf()
        TMPFB = TMPF.bitcast(BF16)
        PTm = [MISC[:, 3072:4096], MISC[:, 4096:5120], TMPFB[:, 3, :]]
        PTSm = [TMPFB[:, 1, 0:512], TMPFB[:, 1, 512:1024], TMPFB[:, 2, 0:512], TMPFB[:, 2, 512:1024]]

        def alias_buf(srcs):
            b_ = arena.buf()
            for sb_ in srcs:
                if sb_.w is not None and b_.r.get(sb_.w[0], 0) < sb_.w[1]:
                    b_.r[sb_.w[0]] = sb_.w[1]
                for s_, v_ in sb_.r.items():
                    if b_.r.get(s_, 0) < v_:
                        b_.r[s_] = v_
            return b_
        PTb = [arena.buf(), arena.buf(), alias_buf([TMPFb[3]])]
        PTSb = [alias_buf([TMPFb[1]]), alias_buf([TMPFb[1]]), alias_buf([TMPFb[2]]), alias_buf([TMPFb[2]])]
        S = wsc[l]
        tr.dma("sp", [(KAT[:, h, :], kaT[h]) for h in range(2)], p1b, [katb])
        tr.dma("sp", [(VA[:, :, :], va.rearrange("(m p) c -> p m c", p=128))], p1b, [vab])
        ODS = [((PS[4], PSb[4]), (PS[5], PSb[5])), ((PTBF[0], PTBb[0]), (PTBF[1], PTBb[1]))]
        npair = 0
        tr.dma("sp", [(QA, qaT[:, :, 0:G].rearrange("h p t -> p h t"))], [p1b[0]], [qab])
        for g in range(NG):
            g0 = g * G
            half = g // 8
            pre_w = [wload(S["out_tm"][0, seg], wbuf[l]["out_tm"], (8, 512)) for seg in range(2)]
            for h in range(6):
                kv = h // 3
                (Oap, Ob_), (Dap, Db_) = ODS[h % 2]

                def st_(p):
                    pr = p % 2
                    pe([MM(PSP[pr][:, 0:512], KAT[:, kv, (2 * p) * 128:(2 * p + 1) * 128], QA[:, h, :], True, True),
                        MM(PSP[pr][:, 512:1024], KAT[:, kv, (2 * p + 1) * 128:(2 * p + 2) * 128], QA[:, h, :], True, True)],
                       [katb, qab], [PSb[2 * pr], PSb[2 * pr + 1]])
                st_(0)
                bufs = {}
                for p in range(34):
                    if p + 1 < 32:
                        st_(p + 1)
                    if p < 32:
                        pr = p % 2
                        pt, ptb = PTm[npair % 3], PTb[npair % 3]
                        pts, ptsb = PTSm[npair % 4], PTSb[npair % 4]
                        npair += 1
                        bufs[p] = (pt, ptb, pts, ptsb)
                        if (2 * p) // 32 != half:
                            act([I("activation", out=pt, in_=PSP[pr][:, :], func=AF.Exp, bias=FLAGS[:, 1:2], scale=SCALE)],
                                [PSb[2 * pr], PSb[2 * pr + 1], CONSTb], [ptb])
                        else:
                            act([I("activation", out=pt, in_=PSP[pr][:, :], func=AF.Exp, scale=SCALE)], [PSb[2 * pr], PSb[2 * pr + 1]], [ptb])
                        dve([I("tensor_tensor", out=pts, in0=pt[:, 0:512], in1=pt[:, 512:1024], op=ALU.add)], [ptb], [ptsb])
                    if 1 <= p <= 32:
                        q = p - 1
                        pt, ptb, _, _ = bufs[q]
                        ins = []
                        for q2 in range(2):
                            kt = 2 * q + q2
                            ins.append(MM(Oap[:, :], VA[:, kt, kv * 128:(kv + 1) * 128], pt[:, q2 * 512:(q2 + 1) * 512], kt == 0, kt == 63))
                        pe(ins, [ptb, vab], [Ob_])
                    if 2 <= p <= 33:
                        q = p - 2
                        _, _, pts, ptsb = bufs[q]
                        pe([MM(Dap[:, :], ONESB[:, :], pts, q == 0, q == 31)], [ptsb, CONSTb], [Db_])
                R = TMPF[:, 0, :]
                Rb = TMPFb[0]
                dve([I("reciprocal", out=R, in_=Dap[:, :])], [Db_], [Rb])
                dve([I("tensor_tensor", out=HT[:, h, :], in0=Oap[:, :], in1=R, op=ALU.mult)], [Ob_, Rb], HTb)
            if g + 1 < NG:
                tr.dma("sp", [(QA, qaT[:, :, g0 + G:g0 + 2 * G].rearrange("h p t -> p h t"))], [p1b[g + 1]], [qab])
            tr.dma("sp", [(HT[:, 6:16, :], mixbc[:, :, g0:g0 + G].rearrange("h p t -> p h t"))], [mixb[g // 4]], HTb)
            for nb in range(4):
                for seg in range(2):
                    if nb == 0:
                        wv, wb_ = pre_w[seg]
                    else:
                        wv, wb_ = wload(S["out_tm"][nb, seg], wbuf[l]["out_tm"], (8, 512))
                    for t in range(4):
                        bk = (4 * nb + t) % 6
                        tm_matmul(PS[bk][:, :], PSb[bk], lambda c, t=t: HT[:, c, t * 128:(t + 1) * 128], [HTb[t]], wv, wb_, 512, seg * 8, 8, 16)
                for t in range(4):
                    bk = (4 * nb + t) % 6
                    tt = 4 * g + t
                    xs = XT[:, t, nb * 512:(nb + 1) * 512]
                    tr.dma("sp", [(xs, xsrc[tt * 128:(tt + 1) * 128, nb * 512:(nb + 1) * 512])], [xb[g]], [XTb[t]])
                    dve([I("tensor_tensor", out=xs, in0=PS[bk][:, :], in1=xs, op=ALU.add)], [PSb[bk], XTb[t]], [XTb[t]])
            for t in range(4):
                tt = 4 * g + t
                tr.dma("pool", [(xres[tt * 128:(tt + 1) * 128, :], XT[:, t, :])], [XTb[t]], [xb[g]])
        for ab_, k_ in ((PTb[2], 3), (PTSb[0], 1), (PTSb[1], 1), (PTSb[2], 2), (PTSb[3], 2)):
            evs = list(ab_.r.items()) + ([ab_.w] if ab_.w is not None else [])
            for s_, v_ in evs:
                if TMPFb[k_].r.get(s_, 0) < v_:
                    TMPFb[k_].r[s_] = v_

    def phase3(l, conv_chunks):
        arena.begin()
        S = wsc[l]
        AT = BIG[:, 0:22528].rearrange("p (j t) -> p j t", j=44)
        KXTs = [BIG[:, 26624:27648].rearrange("p (h m) -> p h m", h=4), MISC[:, 2048:3072].rearrange("p (h m) -> p h m", h=4)]
        VXs = [BIG[:, 27648:28672].rearrange("p (m c) -> p m c", m=2), MISC[:, 3072:4096].rearrange("p (m c) -> p m c", m=2)]
        QX = BIG[:, 28672:30720].rearrange("p (h t) -> p h t", h=4)
        OX = BIG[:, 30720:32768].rearrange("p (h t) -> p h t", h=4)
        atb = [arena.buf() for i in range(6)]
        kxbs, vxbs = [arena.buf(), arena.buf()], [arena.buf(), arena.buf()]
        qxb, oxb = arena.buf(), arena.buf()
        HB = [BIG[:, 22528:24576], BIG[:, 24576:26624], BIG[:, 28672:30720], BIG[:, 30720:32768]]
        HBb = [arena.buf(), arena.buf(), qxb, oxb]
        PTm = [MISC[:, i * 512:(i + 1) * 512] for i in range(4)]
        PTb = [arena.buf() for _ in range(4)]
        npt = 0
        load_gain(GB, GBb, ln_ffn[l:l + 1, :])
        load_gain(GA, GAb, ln_mem[l:l + 1, :])
        for half in range(2):
            for mt in range(2):
                tr.dma("sp", [(XT[:, mt, :], mem_in[half * 256 + mt * 128:half * 256 + (mt + 1) * 128, :])], [], [XTb[mt]])
            norm_stage_a([XT[:, mt, :] for mt in range(2)], XTb, GA, GAb, HB, HBb)
            norm_stage_b(2, HB, HBb, HT, HTb)
            for b in range(2):
                wv, wb_ = wload(S["ckv_fm"][b], wbuf[l]["ckv_fm"], (16, 256))
                for jj in range(2):
                    h = 2 * b + jj
                    bk = h % 6
                    fm_matmul(PS[bk][:, 0:256], PSb[bk], wv, wb_, jj * 128, lambda c: HT[:, c, 0:256], [HTb[0], HTb[1]])
                    copy_any(KXTs[half][:, h, :], PS[bk][:, 0:256], [PSb[bk]], [kxbs[half]])
            for seg in range(2):
                wv, wb_ = wload(S["ckv_tm"][0, seg], wbuf[l]["ckv_tm"], (8, 512))
                for mt in range(2):
                    tm_matmul(PS[4 + mt][:, :], PSb[4 + mt], lambda c, mt=mt: HT[:, c, mt * 128:(mt + 1) * 128], [HTb[mt]], wv, wb_, 512, seg * 8, 8, 16)
            for mt in range(2):
                copy_any(VXs[half][:, mt, :], PS[4 + mt][:, :], [PSb[4 + mt]], [vxbs[half]])
        load_gain(GA, GAb, ln_cross[l:l + 1, :])

        def stage_x(g):
            for t in range(4):
                tt = 4 * g + t
                tr.dma("sp", [(XT[:, t, :], xres[tt * 128:(tt + 1) * 128, :])], [xb[g]], [XTb[t]])
            norm_stage_a([XT[:, t, :] for t in range(4)], XTb, GA, GAb, HB, HBb)

        def stage_xb(g):
            norm_stage_b(4, HB, HBb, HT, HTb)

        stage_x(0)
        stage_xb(0)
        for g in range(NG):
            g0 = g * G
            half = g // 8
            KXT, VX, kxb, vxb = KXTs[half], VXs[half], kxbs[half], vxbs[half]
            cpieces = []
            if conv_chunks and g < len(conv_chunks):
                for (lk, key, prs) in conv_chunks[g]:
                    for i_ in range(0, len(prs), 2):
                        cpieces.append((lk, key, prs[i_:i_ + 2]))

            def emit_conv_pieces(k, n):
                m = (len(cpieces) + n - 1) // n
                for (lk, key, prs) in cpieces[k * m:(k + 1) * m]:
                    tr.dma("pool", prs, reads=[], writes=[wbuf[lk][key]])
            emit_conv_pieces(0, 6)
            for b in range(2):
                wv, wb_ = wload(S["cq_fm"][b], wbuf[l]["cq_fm"], (16, 256))
                for jj in range(2):
                    h = 2 * b + jj
                    bk = h % 2
                    fm_matmul(PS[bk][:, :], PSb[bk], wv, wb_, jj * 128, lambda c: HT[:, c, :], HTb)
                    copy_any(QX[:, h, :], PS[bk][:, :], [PSb[bk]], [qxb])
            for h in range(4):
                bo, bd = 2 + 2 * (h % 2), 3 + 2 * (h % 2)
                pts = []
                for mt in range(2):
                    bs = mt
                    pe([MM(PS[bs][:, :], KXT[:, h, mt * 128:(mt + 1) * 128], QX[:, h, :], True, True)], [kxb, qxb], [PSb[bs]])
                    pt = PTm[npt % 4]
                    ptb = PTb[npt % 4]
                    npt += 1
                    act([I("activation", out=pt, in_=PS[bs][:, :], func=AF.Exp, scale=SCALE)], [PSb[bs]], [ptb])
                    pts.append((pt, ptb))
                ins = [MM(PS[bo][:, :], VX[:, mt, h * 128:(h + 1) * 128], pts[mt][0], mt == 0, mt == 1) for mt in range(2)]
                ins += [MM(PS[bd][:, :], ONESB[:, :], pts[mt][0], mt == 0, mt == 1) for mt in range(2)]
                pe(ins, [pts[0][1], pts[1][1], vxb, CONSTb], [PSb[bo], PSb[bd]])
                R = TMPF[:, h % 2, :]
                Rb = TMPFb[h % 2]
                dve([I("reciprocal", out=R, in_=PS[bd][:, :])], [PSb[bd]], [Rb])
                dve([I("tensor_tensor", out=OX[:, h, :], in0=PS[bo][:, :], in1=R, op=ALU.mult)], [PSb[bo], Rb], [oxb])
            for nb in range(4):
                wv, wb_ = wload(S["co_tm"][nb, 0], wbuf[l]["co_tm"], (8, 512))
                for t in range(4):
                    bk = (4 * nb + t) % 6
                    tm_matmul(PS[bk][:, :], PSb[bk], lambda c, t=t: OX[:, c, t * 128:(t + 1) * 128], [oxb], wv, wb_, 512, 0, 4, 4)
                    xs = XT[:, t, nb * 512:(nb + 1) * 512]
                    dve([I("tensor_tensor", out=xs, in0=PS[bk][:, :], in1=xs, op=ALU.add)], [PSb[bk], XTb[t]], [XTb[t]])
            for t in range(4):
                tt = 4 * g + t
                tr.dma("pool", [(xres[tt * 128:(tt + 1) * 128, :], XT[:, t, :])], [XTb[t]], [xb[g]])
            emit_conv_pieces(1, 6)
            norm_stage_a([XT[:, t, :] for t in range(4)], XTb, GB, GBb, HB, HBb)
            norm_stage_b(4, HB, HBb, HT, HTb)
            for j in range(44):
                wv, wb_ = wload(S["ffi_fm"][j], wbuf[l]["ffi_fm"], (16, 256))
                bg, bu = 2 * (j % 3), 2 * (j % 3) + 1
                fm_matmul(PS[bg][:, :], PSb[bg], wv, wb_, 0, lambda c: HT[:, c, :], HTb)
                fm_matmul(PS[bu][:, :], PSb[bu], wv, wb_, 128, lambda c: HT[:, c, :], HTb)
                sg = TMPF[:, 2 + j % 2, :]
                sgb = TMPFb[2 + j % 2]
                act([I("activation", out=sg, in_=PS[bg][:, :], func=AF.Silu)], [PSb[bg]], [sgb])
                dve([I("tensor_tensor", out=AT[:, j, :], in0=PS[bu][:, :], in1=sg, op=ALU.mult)], [PSb[bu], sgb], [atb[j // 8]])
            if g + 1 < NG:
                stage_x(g + 1)
            for t in range(4):
                tt = 4 * g + t
                tr.dma("pool", [(TMPF[:, t, :], xres[tt * 128:(tt + 1) * 128, 0:512])], [xb[g]], [TMPFb[t]])
            for nb in range(4):
                for seg in range(6):
                    nch = 8 if seg < 5 else 4
                    wv, wb_ = wload(S["ffo_tm"][nb, seg], wbuf[l]["ffo_tm"], (8, 512))
                    for t in range(4):
                        bk = (4 * nb + t) % 6
                        tm_matmul(PS[bk][:, :], PSb[bk], lambda c, t=t: AT[:, c, t * 128:(t + 1) * 128], [atb[seg]], wv, wb_, 512, seg * 8, nch, 44)
                if nb == 1 and g + 1 < NG:
                    stage_xb(g + 1)
                for t in range(4):
                    bk = (4 * nb + t) % 6
                    tt = 4 * g + t
                    xs = TMPF[:, t, :]
                    dve([I("tensor_tensor", out=xs, in0=PS[bk][:, :], in1=xs, op=ALU.add)], [PSb[bk], TMPFb[t]], [TMPFb[t]])
                    tr.dma("pool", [(xres[tt * 128:(tt + 1) * 128, nb * 512:(nb + 1) * 512], xs)], [TMPFb[t]], [xb[g]])
                    if nb + 1 < 4:
                        tr.dma("pool", [(xs, xres[tt * 128:(tt + 1) * 128, (nb + 1) * 512:(nb + 2) * 512])], [xb[g]], [TMPFb[t]])
                emit_conv_pieces(2 + nb, 6)

    def phase4():
        arena.begin()
        load_gain(GB, GBb, ln_final[0:1, :])
        jbs = [arena.buf(), arena.buf()]
        for tt in range(T // 128):
            t = tt % 4
            g = tt // 4
            xt = XT[:, t, :]
            tr.dma("sp", [(xt, xres[tt * 128:(tt + 1) * 128, :])], [xb[g]], [XTb[t]])
            sv, sbuf_ = stat_slot()
            junk = BIG[:, (tt % 2) * 2048:(tt % 2 + 1) * 2048]
            jb = jbs[tt % 2]
            act([I("activation", out=junk, in_=xt, func=AF.Square, accum_out=sv[:, 0:1])], [XTb[t]], [jb, sbuf_])
            act([I("activation", out=sv[:, 1:2], in_=sv[:, 0:1], func=AF.Sqrt, bias=EPS, scale=1.0 / D)], [sbuf_], [sbuf_])
            dve([I("reciprocal", out=sv[:, 2:3], in_=sv[:, 1:2])], [sbuf_], [sbuf_])
            dve([I("scalar_tensor_tensor", out=xt, in0=xt, scalar=sv[:, 2:3], in1=GB[:, :], op0=ALU.mult, op1=ALU.mult)],
                [XTb[t], sbuf_, GBb], [XTb[t]])
            tr.dma("pool", [(y_out[tt * 128:(tt + 1) * 128, :], xt)], [XTb[t]], [xb[g]], is_output=True)

    for key, prs in conv_ops(0):
        emit_conv(0, key, prs)
    setup()
    for l in range(nlayers):
        phase1(l)
        phase2a(l)
        phase2b(l)
        chunks = None
        if l + 1 < nlayers:
            allp = []
            for key, prs in conv_ops(l + 1):
                for i in range(0, len(prs), 16):
                    allp.append((l + 1, key, prs[i:i + 16]))
            per = (len(allp) + NG - 1) // NG
            chunks = [allp[i * per:(i + 1) * per] for i in range(NG)]
        phase3(l, chunks)
    phase4()
    build.stats = {k: len(v) for k, v in tr.ops.items()}

    with nc.Block() as block:
        @block.sync
        def _(e):
            tr.replay("sp", e, sems)

        @block.scalar
        def _(e):
            tr.replay("act", e, sems)

        @block.tensor
        def _(e):
            tr.replay("pe", e, sems)

        @block.vector
        def _(e):
            tr.replay("dve", e, sems)

        @block.gpsimd
        def _(e):
            tr.replay("pool", e, sems)
            done = {}
            for s, v in tr.out_events:
                done[s] = max(done.get(s, 0), v)
            for s, v in done.items():
                e.wait_ge(sems[s], v)
    st.close()
    return nc


def _t5_bucket(rel):
    nb = 16
    max_exact = 8
    ret = np.where(rel > 0, nb, 0)
    n = np.abs(rel)
    large = max_exact + (np.log(np.maximum(n, 1).astype(np.float32) / max_exact)
                         / np.float32(math.log(1024 / max_exact)) * (nb - max_exact)).astype(np.int32)
    large = np.minimum(large, nb - 1)
    return ret + np.where(n < max_exact, n, large)


def _rope_table(positions):
    inv = (10000.0 ** (-np.arange(0, 64, 2, dtype=np.float32) / 64)).astype(np.float32)
    row = (positions // 64).astype(np.float32)
    col = (positions % 64).astype(np.float32)
    ar = row[:, None] * inv
    ac = col[:, None] * inv
    cr, sr, cc, sc = np.cos(ar), np.sin(ar), np.cos(ac), np.sin(ac)
    C = np.concatenate([cr, cr, cc, cc], axis=1)
    S = np.concatenate([-sr, sr, -sc, sc], axis=1)
    return np.concatenate([C, S], axis=1).astype(np.float32)


def _host_tables():
    m = np.arange(512)
    rel = 256 - m
    oh = np.zeros((32, 3, 512), np.float32)
    for gt, d in enumerate((1, 4, 16)):
        bk = _t5_bucket((rel * d).astype(np.int32))
        oh[bk, gt, m] = 1.0
    mk = np.zeros((10, 512), np.float32)
    mk[0:4] = (np.abs(rel) <= 128).astype(np.float32)
    mk[4:10] = (np.abs(rel) <= 64).astype(np.float32)
    ident = np.eye(128, dtype=np.float32)
    jmat = np.ascontiguousarray(ident[::-1])
    return oh.reshape(32, 1536), mk, ident, jmat


def _flags(cross):
    p = np.arange(128)
    f = np.zeros((128, 8), np.float32)
    mlo = (p >= 64).astype(np.float32)
    mhi = (p < 64).astype(np.float32)
    f[:, 0] = cross
    f[:, 1] = 0.0 if cross else -30000.0
    f[:, 2] = mlo
    f[:, 3] = mhi
    f[:, 4] = np.maximum(mlo, cross)
    f[:, 5] = np.maximum(mhi, cross)
    return f


_NC_CACHE = {}


def make_in_maps(inputs):
    f32 = lambda a: np.ascontiguousarray(np.asarray(a, dtype=np.float32))
    xp, xs = f32(inputs["x_prompt"]), f32(inputs["x_sample"])
    mp, ms = f32(inputs["mem_prompt"]), f32(inputs["mem_sample"])
    oh, mk, ident, jmat = _host_tables()
    shared = {k: f32(inputs[k]) for k in ("ln_mix", "w_in", "q_norm_a", "k_norm_a", "rel_bias", "w_out", "ln_cross", "ln_mem",
                                         "w_cq", "w_ckv", "w_co", "ln_ffn", "w_ffn_in", "w_ffn_out")}
    shared["sink_b"] = f32(inputs["sink_b"]).reshape(1, 16)
    shared["ln_final"] = f32(inputs["ln_final"]).reshape(1, D)
    shared.update(oh=oh, mk=mk, ident=ident, jmat=jmat)
    rope_p = _rope_table(np.arange(T))
    rope_s = _rope_table(np.concatenate([np.arange(HALF), np.arange(HALF)]))
    maps = []
    for c in range(8):
        m = dict(shared)
        if c < 2:
            m["x"] = xp[c]
            m["mem"] = np.concatenate([mp[c], mp[c]], axis=0)
            m["rope"] = rope_p
            m["flags"] = _flags(1.0)
        elif c < 6:
            i = 2 * (c - 2)
            m["x"] = np.concatenate([xs[i], xs[i + 1]], axis=0)
            m["mem"] = np.concatenate([ms[i], ms[i + 1]], axis=0)
            m["rope"] = rope_s
            m["flags"] = _flags(0.0)
        else:
            m["x"] = np.zeros((T, D), np.float32)
            m["mem"] = np.zeros((512, D), np.float32)
            m["rope"] = rope_s
            m["flags"] = _flags(0.0)
        maps.append(m)
    return maps


def kernel(**inputs):
    if "nc" not in _NC_CACHE:
        _NC_CACHE["nc"] = build()
    nc = _NC_CACHE["nc"]
    maps = make_in_maps(inputs)
    res = run_bass_kernel_spmd(nc, maps, core_ids=list(range(8)))
    ys = [np.asarray(r["y"], dtype=np.float32) for r in res.results]
    y_prompt = np.stack([ys[0], ys[1]], axis=0)
    y_sample = np.stack([ys[2 + i // 2][(i % 2) * HALF:(i % 2 + 1) * HALF] for i in range(8)], axis=0)
    return (y_prompt, y_sample)
```

```python
import math
from contextlib import ExitStack
import numpy as np
import concourse.bass as bass
import concourse.mybir as mybir
from concourse.bass_utils import run_bass_kernel_spmd

F32 = mybir.dt.float32
BF16 = mybir.dt.bfloat16
AF = mybir.ActivationFunctionType
ALU = mybir.AluOpType

T = 8192
D = 2048
G = 512
NG = T // G
HALF = 4096
PAD = 1024
PADR = 1024 + 64
DFF = 5632
SCALE = 128 ** -0.5
EPS = 1e-6
NWS = 4


class Buf:
    __slots__ = ("w", "r")

    def __init__(self):
        self.w = None
        self.r = {}


class Tracker:
    def __init__(self):
        self.ops = {k: [] for k in ("sp", "act", "pe", "dve", "pool")}
        self.seen = {k: {} for k in self.ops}
        self.count = {"act": 0, "pe": 0, "dve": 0}
        self.semid = {"act": 0, "pe": 1, "dve": 2}
        self.nring = 20
        self.ring = {"sp": [3 + i for i in range(self.nring)], "pool": [3 + self.nring + i for i in range(self.nring)]}
        self.ringval = {}
        self.ringpos = {"sp": 0, "pool": 0}
        self.nsem = 3 + 2 * self.nring
        self.out_events = []

    def _needs(self, eng, reads, writes):
        need = {}

        def add(ev):
            if ev is None:
                return
            s, v = ev
            if need.get(s, 0) < v:
                need[s] = v

        for b in reads:
            add(b.w)
        for b in writes:
            add(b.w)
            for s, v in b.r.items():
                add((s, v))
        seen = self.seen[eng]
        wl = []
        for s, v in need.items():
            if eng == "pe" and s == self.semid["pe"]:
                continue
            if seen.get(s, 0) >= v:
                continue
            seen[s] = v
            wl.append((s, v))
        return wl

    def _post(self, ev, reads, writes):
        s, v = ev
        for b in reads:
            if b.r.get(s, 0) < v:
                b.r[s] = v
        for b in writes:
            b.w = ev
            b.r = {}

    def op(self, eng, fn, reads=(), writes=()):
        wl = self._needs(eng, reads, writes)
        self.count[eng] += 1
        ev = (self.semid[eng], self.count[eng])
        self.ops[eng].append((wl, fn, ev))
        self._post(ev, reads, writes)
        return ev

    def dma(self, q, pairs, reads=(), writes=(), is_output=False):
        i = self.ringpos[q]
        self.ringpos[q] = (i + 1) % self.nring
        s = self.ring[q][i]
        prev = self.ringval.get(s, 0)
        wl = self._needs(q, reads, writes)
        seen = self.seen[q]
        if prev > 0 and seen.get(s, 0) < prev:
            seen[s] = prev
            wl.append((s, prev))
        newv = prev + 16 * len(pairs)
        self.ringval[s] = newv
        ev = (s, newv)
        self.ops[q].append((wl, pairs, ev))
        self._post(ev, reads, writes)
        if is_output:
            self.out_events.append(ev)
        return ev

    def replay(self, eng, e, sems):
        for wl, fn, ev in self.ops[eng]:
            for s, v in wl:
                e.wait_ge(sems[s], v)
            if eng in ("sp", "pool"):
                for o, i_ in fn:
                    e.dma_start(out=o, in_=i_).then_inc(sems[ev[0]], 16)
            else:
                ins = None
                for name, kw in fn:
                    ins = getattr(e, name)(**kw)
                ins.then_inc(sems[ev[0]], 1)


def I(name, **kw):
    return (name, kw)


def MM(out, lhsT, rhs, start, stop):
    return ("matmul", dict(out=out, lhsT=lhsT, rhs=rhs, start=start, stop=stop))


class Arena:
    def __init__(self):
        self.live = []
        self.merged = {}

    def begin(self):
        m = dict(self.merged)
        for b in self.live:
            if b.w is not None:
                s_, v_ = b.w
                if m.get(s_, 0) < v_:
                    m[s_] = v_
            for s_, v_ in b.r.items():
                if m.get(s_, 0) < v_:
                    m[s_] = v_
        self.merged = m
        self.live = []

    def buf(self):
        b = Buf()
        b.r = dict(self.merged)
        self.live.append(b)
        return b


def build(nlayers=4, dbg=False):
    nc = bass.Bass("TRN2", target_bir_lowering=False)
    tr = Tracker()
    arena = Arena()

    def din(name, shape, dt=F32):
        return nc.dram_tensor(name, list(shape), dt, kind="ExternalInput").ap()

    def dscr(name, shape, dt, force_internal=False):
        kind = "ExternalOutput" if (dbg and not force_internal) else "Internal"
        return nc.dram_tensor(name, list(shape), dt, kind=kind).ap()

    L = 4
    x_in = din("x", [T, D])
    mem_in = din("mem", [512, D])
    ln_mix = din("ln_mix", [L, D]); w_in = din("w_in", [L, D, 3840])
    q_norm_a = din("q_norm_a", [L, 128]); k_norm_a = din("k_norm_a", [L, 128])
    sink_b = din("sink_b", [1, 16]); rel_bias = din("rel_bias", [32, 10])
    w_out = din("w_out", [L, D, D]); ln_cross = din("ln_cross", [L, D]); ln_mem = din("ln_mem", [L, D])
    w_cq = din("w_cq", [L, D, 512]); w_ckv = din("w_ckv", [L, D, 1024]); w_co = din("w_co", [L, 512, D])
    ln_ffn = din("ln_ffn", [L, D]); w_ffn_in = din("w_ffn_in", [L, D, 2 * DFF]); w_ffn_out = din("w_ffn_out", [L, DFF, D])
    ln_final = din("ln_final", [1, D])
    rope_in = din("rope", [T, 256])
    oh_in = din("oh", [32, 3 * 512])
    mk_in = din("mk", [10, 512])
    flags_in = din("flags", [128, 8])
    ident_in = din("ident", [128, 128])
    jmat_in = din("jmat", [128, 128])
    y_out = nc.dram_tensor("y", [T, D], F32, kind="ExternalOutput").ap()

    xres = dscr("xres", [T, D], F32)
    qaT = dscr("qaT", [6, 128, T], BF16); kaT = dscr("kaT", [2, 128, T], BF16); va = dscr("va", [T, 256], BF16)
    qbT = dscr("qbT", [4, 128, T], BF16); kbT = dscr("kbT", [2, 128, PAD + T + PAD], BF16)
    vb = dscr("vb", [PAD + T + PADR, 256], BF16)
    qcT = dscr("qcT", [6, 128, T], BF16); kcT = dscr("kcT", [3, 128, PAD + T + PAD], BF16)
    vc = dscr("vc", [PAD + T + PADR, 384], BF16)
    mixbc = dscr("mixbc", [10, 128, T], BF16)
    evec = dscr("evec", [10, 512], F32)
    wsc = []
    for l in range(nlayers):
        d_ = {}
        d_["in_fm"] = dscr(f"s_in_fm{l}", [8, 128, 16, 256], BF16, True)
        d_["cq_fm"] = dscr(f"s_cq_fm{l}", [2, 128, 16, 256], BF16, True)
        d_["ckv_fm"] = dscr(f"s_ckv_fm{l}", [2, 128, 16, 256], BF16, True)
        d_["ffi_fm"] = dscr(f"s_ffi_fm{l}", [44, 128, 16, 256], BF16, True)
        d_["in_tm"] = dscr(f"s_in_tm{l}", [4, 2, 128, 8, 512], BF16, True)
        d_["out_tm"] = dscr(f"s_out_tm{l}", [4, 2, 128, 8, 512], BF16, True)
        d_["ckv_tm"] = dscr(f"s_ckv_tm{l}", [1, 2, 128, 8, 512], BF16, True)
        d_["co_tm"] = dscr(f"s_co_tm{l}", [4, 1, 128, 8, 512], BF16, True)
        d_["ffo_tm"] = dscr(f"s_ffo_tm{l}", [4, 6, 128, 8, 512], BF16, True)
        wsc.append(d_)
    wbuf = [{k: Buf() for k in wsc[l]} for l in range(nlayers)]

    st = ExitStack()

    def sb(name, shape, dt):
        return st.enter_context(nc.sbuf_tensor(name, list(shape), dt))

    WS = sb("WS", [128, NWS, 4096], BF16)
    XT = sb("XT", [128, 4, 2048], F32)
    HT = sb("HT", [128, 16, 512], BF16)
    BIG = sb("BIG", [128, 32768], BF16)
    MISC = sb("MISC", [128, 5120], BF16)
    TMPF = sb("TMPF", [128, 4, 512], F32)
    GA = sb("GA", [128, 2048], F32)
    GB = sb("GB", [128, 2048], F32)
    EB = sb("EB", [128, 10, 384], F32)
    ROPE = sb("ROPE", [128, 8, 256], F32)
    G01 = sb("G01", [128, 2, 512], F32)
    IDB = sb("IDB", [128, 128], BF16)
    ONESB = sb("ONESB", [128, 128], BF16)
    JF = sb("JF", [128, 128], F32)
    STAT = sb("STAT", [128, 64], F32)
    ESINK = sb("ESINK", [128, 16], F32)
    FLAGS = sb("FLAGS", [128, 8], F32)
    SETUP = TMPF[0:32, :, :].rearrange("p a b -> p (a b)")[:, 0:10 + 3 * 512]
    E3 = GB[0:10, :].rearrange("p (a b) -> p a b", a=4)
    PSP = [st.enter_context(nc.psum_tensor(f"PSP{i}", [128, 1024], F32)) for i in range(3)]
    PS = [PSP[i // 2][:, (i % 2) * 512:(i % 2 + 1) * 512] for i in range(6)]
    PTB = [st.enter_context(nc.psum_tensor(f"PTB{i}", [128, 1024], BF16)) for i in range(2)]
    PTBF = [PTB[i].bitcast(F32) for i in range(2)]
    sems = [st.enter_context(nc.semaphore(f"s{i}")) for i in range(tr.nsem)]

    WSb = [Buf() for _ in range(NWS)]
    XTb = [Buf() for _ in range(4)]
    HTb = [Buf() for _ in range(4)]
    PSb = [Buf() for _ in range(6)]
    PTBb = [Buf() for _ in range(2)]
    TMPFb = [Buf() for _ in range(4)]
    GAb, GBb, EBb, G01b, CONSTb, ESINKb = Buf(), Buf(), Buf(), Buf(), Buf(), Buf()
    ROPEb = [Buf() for _ in range(8)]
    STATb = [Buf() for _ in range(8)]
    xb = [Buf() for _ in range(NG)]
    p1b = [Buf() for _ in range(NG)]
    padb = Buf()
    mixb = [Buf() for _ in range(4)]

    state = {"ws": 0, "stat": 0, "cp": 0}

    def act(ins, reads, writes):
        return tr.op("act", ins, reads, writes)

    def dve(ins, reads, writes):
        return tr.op("dve", ins, reads, writes)

    def pe(ins, reads, writes):
        return tr.op("pe", ins, reads, writes)

    def copy_any(out, in_, reads, writes):
        state["cp"] += 1
        if state["cp"] % 2:
            return act([I("activation", out=out, in_=in_, func=AF.Copy)], reads, writes)
        return dve([I("tensor_copy", out=out, in_=in_)], reads, writes)

    def wload(src_ap, wb, shape3):
        s = state["ws"]
        state["ws"] = (s + 1) % NWS
        n = shape3[0] * shape3[1]
        view = WS[:, s, 0:n].rearrange("p (a b) -> p a b", a=shape3[0])
        tr.dma("sp", [(view, src_ap)], reads=[wb], writes=[WSb[s]])
        return view, WSb[s]

    def stat_slot():
        s = state["stat"]
        state["stat"] = (s + 1) % 8
        return STAT[:, s * 8:(s + 1) * 8], STATb[s]

    def load_gain(tile, tb, src_row):
        tr.dma("sp", [(tile[:, :], src_row.to_broadcast([128, D]))], reads=[], writes=[tb])

    def norm_stage_a(xaps, xbufs, gain, gainb, hbs, hbbs):
        n = len(xaps)
        sv, sbuf_ = stat_slot()
        for t in range(n):
            act([I("activation", out=hbs[t], in_=xaps[t], func=AF.Square, accum_out=sv[:, t:t + 1])], [xbufs[t]], [hbbs[t], sbuf_])
        act([I("activation", out=sv[:, 4:4 + n], in_=sv[:, 0:n], func=AF.Sqrt, bias=EPS, scale=1.0 / D)], [sbuf_], [sbuf_])
        dve([I("reciprocal", out=sv[:, 4:4 + n], in_=sv[:, 4:4 + n])], [sbuf_], [sbuf_])
        for t in range(n):
            dve([I("scalar_tensor_tensor", out=hbs[t], in0=xaps[t], scalar=sv[:, 4 + t:5 + t], in1=gain[:, :], op0=ALU.mult, op1=ALU.mult)],
                [xbufs[t], sbuf_, gainb], [hbbs[t]])

    def norm_stage_b(n, hbs, hbbs, htdst, htbufs):
        for t in range(n):
            for k in range(2):
                ins = [I("transpose", out=PTB[k][:, j * 128:(j + 1) * 128], in_=hbs[t][:, (k * 8 + j) * 128:(k * 8 + j + 1) * 128], identity=IDB[:, :])
                       for j in range(8)]
                pe(ins, [hbbs[t], CONSTb], [PTBb[k]])
                copy_any(htdst[:, k * 8:(k + 1) * 8, t * 128:(t + 1) * 128], PTB[k][:, :].rearrange("p (j q) -> p j q", j=8),
                         [PTBb[k]], [htbufs[t]])

    def fm_matmul(psum_ap, psb, wview, wb, joff, rhs_fn, rhs_bufs, kc=16):
        ins = [MM(psum_ap, wview[:, c, joff:joff + 128], rhs_fn(c), c == 0, c == kc - 1) for c in range(kc)]
        pe(ins, [wb] + list(rhs_bufs), [psb])

    def tm_matmul(psum_ap, psb, lhs_fn, lhs_bufs, wview, wb, ncols, c0, nch, ktot):
        ins = [MM(psum_ap, lhs_fn(c0 + cc), wview[:, cc, 0:ncols], (c0 + cc) == 0, (c0 + cc) == ktot - 1) for cc in range(nch)]
        pe(ins, [wb] + list(lhs_bufs), [psb])

    def conv_ops(l):
        ops = []
        S = wsc[l]

        def fm_range(dst, b0, nb, j0, jw, src, col0):
            prs = []
            for c in range(16):
                o = dst[b0:b0 + nb, :, c, j0:j0 + jw].rearrange("b p j -> p b j")
                i_ = src[c * 128:(c + 1) * 128, col0:col0 + nb * jw].rearrange("p (b j) -> p b j", j=jw)
                prs.append((o, i_))
            return prs

        def tm_range(dst, nb0, nnb, j0, jw, src, col0, kch):
            prs = []
            for c in range(kch):
                o = dst[nb0:nb0 + nnb, c // 8, :, c % 8, j0:j0 + jw].rearrange("b p j -> p b j")
                i_ = src[c * 128:(c + 1) * 128, col0:col0 + nnb * jw].rearrange("p (b j) -> p b j", j=jw)
                prs.append((o, i_))
            return prs

        wi = w_in[l]
        ops.append(("in_tm", tm_range(S["in_tm"], 0, 2, 0, 512, wi, 0, 16)
                    + tm_range(S["in_tm"], 2, 1, 0, 256, wi, 1024, 16)
                    + tm_range(S["in_tm"], 2, 1, 256, 256, wi, 2048, 16)
                    + tm_range(S["in_tm"], 3, 1, 0, 384, wi, 3456, 16)))
        ops.append(("in_fm", fm_range(S["in_fm"], 0, 3, 0, 256, wi, 1280)
                    + fm_range(S["in_fm"], 3, 4, 0, 256, wi, 2304)
                    + fm_range(S["in_fm"], 7, 1, 0, 128, wi, 3328)))
        ops.append(("out_tm", tm_range(S["out_tm"], 0, 4, 0, 512, w_out[l], 0, 16)))
        ops.append(("ckv_fm", fm_range(S["ckv_fm"], 0, 2, 0, 256, w_ckv[l], 0)))
        ops.append(("ckv_tm", tm_range(S["ckv_tm"], 0, 1, 0, 512, w_ckv[l], 512, 16)))
        ops.append(("cq_fm", fm_range(S["cq_fm"], 0, 2, 0, 256, w_cq[l], 0)))
        ops.append(("co_tm", tm_range(S["co_tm"], 0, 4, 0, 512, w_co[l], 0, 4)))
        ops.append(("ffi_fm", fm_range(S["ffi_fm"], 0, 44, 0, 128, w_ffn_in[l], 0)
                    + fm_range(S["ffi_fm"], 0, 44, 128, 128, w_ffn_in[l], DFF)))
        ops.append(("ffo_tm", tm_range(S["ffo_tm"], 0, 4, 0, 512, w_ffn_out[l], 0, 44)))
        return ops

    def emit_conv(l, key, prs):
        for i in range(0, len(prs), 16):
            tr.dma("pool", prs[i:i + 16], reads=[], writes=[wbuf[l][key]])

    def setup():
        tr.dma("pool", [(IDB[:, :], ident_in[:, :])], [], [CONSTb])
        tr.dma("sp", [(JF[:, :], jmat_in[:, :]), (FLAGS[:, :], flags_in[:, :]),
                      (SETUP[:, 0:10], rel_bias[:, :]), (SETUP[:, 10:10 + 1536], oh_in[:, :]),
                      (E3[:, 3, :], mk_in[:, :]),
                      (ESINK[:, :], sink_b[0:1, :].to_broadcast([128, 16]))], [], [CONSTb, ESINKb, GBb] + TMPFb)
        dve([I("memset", ap=ONESB[:, :], constant=1.0)], [], [CONSTb])
        act([I("activation", out=ESINK[:, :], in_=ESINK[:, :], func=AF.Exp)], [ESINKb], [ESINKb])
        zt = XT.bitcast(BF16)
        ztz = zt[:, 0, :] if len(zt.shape) == 3 else zt[:, 0:4096]
        dve([I("memset", ap=XT[:, 0, :], constant=0.0)], [], [XTb[0]])
        prs = []
        for h in range(2):
            prs.append((kbT[h, :, 0:PAD], ztz[:, 0:PAD]))
            prs.append((kbT[h, :, PAD + T:PAD + T + PAD], ztz[:, 0:PAD]))
        for h in range(3):
            prs.append((kcT[h, :, 0:PAD], ztz[:, 0:PAD]))
            prs.append((kcT[h, :, PAD + T:PAD + T + PAD], ztz[:, 0:PAD]))
        for (v_, w_) in ((vb, 256), (vc, 384)):
            for r0 in range(0, PAD, 128):
                prs.append((v_[r0:r0 + 128, :], ztz[:, 0:w_]))
            for r0 in range(0, PADR, 128):
                n = min(128, PADR - r0)
                prs.append((v_[PAD + T + r0:PAD + T + r0 + n, :], ztz[0:n, 0:w_]))
        tr.dma("pool", prs, [XTb[0]], [padb])
        for gt in range(3):
            pe([MM(PS[gt][0:10, :], SETUP[:, 0:10], SETUP[:, 10 + gt * 512:10 + (gt + 1) * 512], True, True)], [CONSTb] + TMPFb, [PSb[gt]])
            act([I("activation", out=E3[:, gt, :], in_=PS[gt][0:10, :], func=AF.Exp)], [PSb[gt]], [EBb, GBb])
            dve([I("tensor_tensor", out=E3[:, gt, :], in0=E3[:, gt, :], in1=E3[:, 3, :], op=ALU.mult)], [EBb, CONSTb, GBb], [EBb, GBb])
        evb = Buf()
        tr.dma("pool", [(evec[0:6, :], E3[0:6, 0, :]), (evec[6:8, :], E3[6:8, 1, :]), (evec[8:10, :], E3[8:10, 2, :])], [EBb, GBb], [evb])
        hk = XT[:, 1, :]
        for h in range(10):
            offs = (-128, 0, 128) if h < 4 else (-64, 64)
            prs = []
            for a, off in enumerate(offs):
                base = 129 - off
                src = bass.AP(tensor=evec.tensor, offset=h * 512 + base, ap=[[1, 128], [1, 128]])
                prs.append((hk[:, a * 128:(a + 1) * 128], src))
            tr.dma("sp", prs, [evb], [XTb[1]])
            n = len(offs) * 128
            bk = 3 + h % 2
            pe([MM(PS[bk][:, 0:n], JF[:, :], hk[:, 0:n], True, True)], [XTb[1], CONSTb], [PSb[bk]])
            dve([I("tensor_copy", out=EB[:, h, 0:n], in_=PS[bk][:, 0:n])], [PSb[bk]], [EBb])

    def phase1(l):
        arena.begin()
        xsrc = x_in if l == 0 else xres
        load_gain(GA, GAb, ln_mix[l:l + 1, :])
        tr.dma("sp", [(G01[:, 0, :].rearrange("p (h d) -> p h d", h=4), q_norm_a[l:l + 1, :].unsqueeze(1).to_broadcast([128, 4, 128])),
                      (G01[:, 1, 0:256].rearrange("p (h d) -> p h d", h=2), q_norm_a[l:l + 1, :].unsqueeze(1).to_broadcast([128, 2, 128])),
                      (G01[:, 1, 256:512].rearrange("p (h d) -> p h d", h=2), k_norm_a[l:l + 1, :].unsqueeze(1).to_broadcast([128, 2, 128]))],
               [], [G01b])
        HB = [BIG[:, i * 2048:(i + 1) * 2048] for i in range(4)]
        HBb = [arena.buf() for _ in range(4)]
        QKST = BIG[:, 8192:12288].rearrange("p (h t) -> p h t", h=8)
        VST = BIG[:, 12288:12288 + 3584].rearrange("p (t c) -> p t c", t=4)
        FMST = BIG[:, 15872:15872 + 7680].rearrange("p (h t) -> p h t", h=15)
        OBs = [MISC[:, i * 512:(i + 1) * 512].rearrange("p (h d) -> p h d", h=4) for i in range(4)]
        obbs = [arena.buf() for _ in range(4)]
        HT2 = BIG[:, 24576:32768].rearrange("p (c t) -> p c t", c=16)
        HT2b = [arena.buf() for _ in range(4)]
        HTs = [(HT, HTb), (HT2, HT2b)]
        qkstb, vstb, fmstb = arena.buf(), arena.buf(), arena.buf()
        S = wsc[l]

        def qk_transposes(nb, t):
            pe([I("transpose", out=PTB[nb][:, h * 128:(h + 1) * 128], in_=OBs[t][:, h, :], identity=IDB[:, :]) for h in range(4)],
               [obbs[t], CONSTb], [PTBb[nb]])
            copy_any(QKST[:, nb * 4:(nb + 1) * 4, t * 128:(t + 1) * 128],
                     PTB[nb][:, 0:512].rearrange("p (h q) -> p h q", h=4), [PTBb[nb]], [qkstb])

        def p1_stage_a(g):
            for t in range(4):
                tt = 4 * g + t
                tr.dma("sp", [(XT[:, t, :], xsrc[tt * 128:(tt + 1) * 128, :])], [xb[g]], [XTb[t]])
            rs = (g % 2) * 4
            tr.dma("sp", [(ROPE[:, rs + t, :], rope_in[(4 * g + t) * 128:(4 * g + t + 1) * 128, :]) for t in range(4)], [],
                   [ROPEb[rs + t] for t in range(4)])
            norm_stage_a([XT[:, t, :] for t in range(4)], XTb, GA, GAb, HB, HBb)

        def p1_stage_b(g):
            htd, htdb = HTs[g % 2]
            norm_stage_b(4, HB, HBb, htd, htdb)

        p1_stage_a(0)
        p1_stage_b(0)
        for g in range(NG):
            g0 = g * G
            HTg, HTgb = HTs[g % 2]
            if g + 1 < NG:
                p1_stage_a(g + 1)
            deferred = []
            for nb in range(4):
                ncols = 384 if nb == 3 else 512
                for seg in range(2):
                    wv, wb_ = wload(S["in_tm"][nb, seg], wbuf[l]["in_tm"], (8, 512))
                    for t in range(4):
                        bk = (4 * nb + t) % 6
                        tm_matmul(PS[bk][:, 0:ncols], PSb[bk], lambda c, t=t: HTg[:, c, t * 128:(t + 1) * 128], [HTgb[t]],
                                  wv, wb_, ncols, seg * 8, 8, 16)
                for (dnb, dt) in deferred:
                    qk_transposes(dnb, dt)
                deferred = []
                for t in range(4):
                    bk = (4 * nb + t) % 6
                    if nb >= 2:
                        dst = VST[:, t, 0:512] if nb == 2 else VST[:, t, 512:896]
                        copy_any(dst, PS[bk][:, 0:ncols], [PSb[bk]], [vstb])
                        continue
                    rb = ROPEb[(g % 2) * 4 + t]
                    RC = ROPE[:, (g % 2) * 4 + t, 0:128]
                    RS = ROPE[:, (g % 2) * 4 + t, 128:256]
                    psv = PS[bk][:, :].rearrange("p (h d) -> p h d", h=4)
                    sv, sbuf_ = stat_slot()
                    junk = TMPF[:, 3, 0:128]
                    for h in range(4):
                        act([I("activation", out=junk, in_=psv[:, h, :], func=AF.Square, accum_out=sv[:, h:h + 1])],
                            [PSb[bk]], [TMPFb[3], sbuf_])
                    act([I("activation", out=sv[:, 4:8], in_=sv[:, 0:4], func=AF.Sqrt, bias=EPS, scale=1.0 / 128)], [sbuf_], [sbuf_])
                    dve([I("reciprocal", out=sv[:, 4:8], in_=sv[:, 4:8])], [sbuf_], [sbuf_])
                    QN = TMPF[:, 0, :].rearrange("p (h d) -> p h d", h=4)
                    T1 = TMPF[:, 1, :].rearrange("p (h d) -> p h d", h=4)
                    T2 = TMPF[:, 2, :].rearrange("p (h d) -> p h d", h=4)
                    for h in range(4):
                        dve([I("scalar_tensor_tensor", out=QN[:, h, :], in0=psv[:, h, :], scalar=sv[:, 4 + h:5 + h],
                               in1=G01[:, nb, h * 128:(h + 1) * 128], op0=ALU.mult, op1=ALU.mult)],
                            [PSb[bk], sbuf_, G01b], [TMPFb[0]])
                    dve([I("tensor_tensor", out=T1, in0=QN, in1=RC.unsqueeze(1).to_broadcast([128, 4, 128]), op=ALU.mult)],
                        [TMPFb[0], rb], [TMPFb[1]])
                    QN5 = TMPF[:, 0, :].rearrange("p (h a b f) -> p h a b f", h=4, a=2, b=2)
                    T25 = TMPF[:, 2, :].rearrange("p (h a b f) -> p h a b f", h=4, a=2, b=2)
                    RS4 = RS.rearrange("p (a b f) -> p a b f", a=2, b=2)
                    for h in range(4):
                        dve([I("tensor_tensor", out=T25[:, h, :, 0, :], in0=QN5[:, h, :, 1, :], in1=RS4[:, :, 0, :], op=ALU.mult)],
                            [TMPFb[0], rb], [TMPFb[2]])
                        dve([I("tensor_tensor", out=T25[:, h, :, 1, :], in0=QN5[:, h, :, 0, :], in1=RS4[:, :, 1, :], op=ALU.mult)],
                            [TMPFb[0], rb], [TMPFb[2]])
                    dve([I("tensor_tensor", out=OBs[t], in0=T1, in1=T2, op=ALU.add)], [TMPFb[1], TMPFb[2]], [obbs[t]])
                    deferred.append((nb, t))
            tr.dma("pool", [(qaT[:, :, g0:g0 + G].rearrange("h p t -> p h t"), QKST[:, 0:6, :]),
                            (kaT[:, :, g0:g0 + G].rearrange("h p t -> p h t"), QKST[:, 6:8, :])], [qkstb], [p1b[g]])
            tr.dma("pool", [(va[g0:g0 + G, :].rearrange("(t p) c -> p t c", p=128), VST[:, :, 0:256]),
                            (vb[PAD + g0:PAD + g0 + G, :].rearrange("(t p) c -> p t c", p=128), VST[:, :, 256:512]),
                            (vc[PAD + g0:PAD + g0 + G, :].rearrange("(t p) c -> p t c", p=128), VST[:, :, 512:896])], [vstb], [p1b[g]])
            if g + 1 < NG:
                p1_stage_b(g + 1)
            for b in range(8):
                wv, wb_ = wload(S["in_fm"][b], wbuf[l]["in_fm"], (16, 256))
                for jj in range(2):
                    fh = 2 * b + jj
                    if fh >= 15:
                        continue
                    bk = fh % 6
                    fm_matmul(PS[bk][:, :], PSb[bk], wv, wb_, jj * 128, lambda c: HTg[:, c, :], HTgb)
                    copy_any(FMST[:, fh, :], PS[bk][:, :], [PSb[bk]], [fmstb])
            tr.dma("pool", [(qbT[:, :, g0:g0 + G].rearrange("h p t -> p h t"), FMST[:, 0:4, :]),
                            (kbT[:, :, PAD + g0:PAD + g0 + G].rearrange("h p t -> p h t"), FMST[:, 4:6, :]),
                            (qcT[:, :, g0:g0 + G].rearrange("h p t -> p h t"), FMST[:, 6:12, :]),
                            (kcT[:, :, PAD + g0:PAD + g0 + G].rearrange("h p t -> p h t"), FMST[:, 12:15, :])], [fmstb], [p1b[g]])

    def phase2a(l):
        arena.begin()
        QT = [BIG[:, 0:2048], BIG[:, 2048:4096]]
        KT = [BIG[:, 4096:8192], BIG[:, 8192:12288]]
        VT = [BIG[:, 12288:16384].rearrange("p (m d) -> p m d", m=32), BIG[:, 16384:20480].rearrange("p (m d) -> p m d", m=32)]
        OST = [BIG[:, 20480:22528], BIG[:, 22528:24576]]
        qtb = [arena.buf(), arena.buf()]
        ktb = [arena.buf(), arena.buf()]
        vtb = [arena.buf(), arena.buf()]
        ostb = [arena.buf(), arena.buf()]
        PTm = [MISC[:, i * 512:(i + 1) * 512] for i in range(4)]
        PTb = [arena.buf() for _ in range(4)]
        cnt = {"q": 0, "kv": 0, "o": 0, "qb": 0, "chunk": 0}

        def run_head(s, qsrc, d, offs, ebh, sink_col, out_mode, gidx, ki):
            p0 = s * 2048
            nt = len(offs)
            off0 = offs[0]
            qi = cnt["q"] % 2
            cnt["q"] += 1
            rd = [p1b[g] for g in range(4 * s, 4 * s + 4)]
            tr.dma("sp", [(QT[qi], qsrc[:, p0:p0 + 2048])], rd, [qtb[qi]])
            QTv = QT[qi].rearrange("p (i d) -> p d i", d=d)
            H = -off0 * d
            KTv = KT[ki][:, 0:2048 + 2 * H].rearrange("p (i d) -> p d i", d=d)
            nqb = 2048 // (128 * d)
            ntm = nqb + nt - 1
            qblocks = [(r, b) for r in range(d) for b in range(nqb)]
            oi = None
            if out_mode == "B":
                oi = cnt["o"] % 2
                cnt["o"] += 1
            descs = []
            for ci in range(4):
                ck = cnt["chunk"]
                cnt["chunk"] += 1
                bo, bd = 2 + 2 * (ck % 2), 3 + 2 * (ck % 2)
                for qq in range(4):
                    r, b = qblocks[ci * 4 + qq]
                    n = cnt["qb"]
                    cnt["qb"] += 1
                    alist = []
                    for a, off in enumerate(offs):
                        lo = p0 + r + d * (128 * b + off)
                        hi = lo + d * 127
                        if hi < 0 or lo >= T:
                            continue
                        alist.append(a)
                    descs.append(dict(ci=ci, ck=ck, bo=bo, bd=bd, qq=qq, r=r, b=b, n=n, bs=n % 4, alist=alist))

            SBK = [PS[0], PS[1], PTBF[0], PTBF[1]]
            SBKb = [PSb[0], PSb[1], PTBb[0], PTBb[1]]

            def emit_qk(dc):
                r, b, bs = dc["r"], dc["b"], dc["bs"]
                ins = []
                for a in dc["alist"]:
                    i0 = 128 * b + offs[a] - off0
                    ins.append(MM(SBK[bs][:, a * 128:(a + 1) * 128], KTv[:, r, i0:i0 + 128], QTv[:, r, b * 128:(b + 1) * 128], True, True))
                pe(ins, [ktb[ki], qtb[qi]], [SBKb[bs]])

            def emit_rest(dc):
                r, b, bs, n, qq, bo, bd, alist = dc["r"], dc["b"], dc["bs"], dc["n"], dc["qq"], dc["bo"], dc["bd"], dc["alist"]
                a_lo, a_hi = alist[0], alist[-1] + 1
                tf = TMPF[:, n % 2, a_lo * 128:a_hi * 128]
                tfb = TMPFb[n % 2]
                act([I("activation", out=tf, in_=SBK[bs][:, a_lo * 128:a_hi * 128], func=AF.Exp, scale=SCALE)], [SBKb[bs]], [tfb])
                pt = PTm[n % 4]
                ptb = PTb[n % 4]
                dve([I("tensor_tensor", out=pt[:, a_lo * 128:a_hi * 128], in0=tf, in1=EB[:, ebh, a_lo * 128:a_hi * 128], op=ALU.mult)],
                    [tfb, EBb], [ptb])
                for a in alist:
                    lo = p0 + r + d * (128 * b + offs[a])
                    hi = lo + d * 127
                    col = None
                    if lo < 0:
                        col = 2
                    elif hi >= T:
                        col = 3
                    elif p0 >= HALF and lo < HALF:
                        col = 0 if hi < HALF else 4
                    elif p0 < HALF and hi >= HALF:
                        col = 0 if lo >= HALF else 5
                    if col is not None:
                        dve([I("tensor_scalar", out=pt[:, a * 128:(a + 1) * 128], in0=pt[:, a * 128:(a + 1) * 128],
                               scalar1=FLAGS[:, col:col + 1], scalar2=None, op0=ALU.mult)], [ptb, CONSTb], [ptb])
                mbase = r * ntm + b
                ins = []
                for a in alist:
                    ins.append(MM(PS[bo][:, qq * 128:(qq + 1) * 128], VT[ki][:, mbase + a, :], pt[:, a * 128:(a + 1) * 128],
                                  a == alist[0], a == alist[-1]))
                for a in alist:
                    ins.append(MM(PS[bd][:, qq * 128:(qq + 1) * 128], ONESB[:, :], pt[:, a * 128:(a + 1) * 128],
                                  a == alist[0], a == alist[-1]))
                pe(ins, [ptb, vtb[ki], CONSTb], [PSb[bo], PSb[bd]])

            def emit_evac(ci, ck, bo, bd):
                if d == 1:
                    dsel = lambda buf: buf[:, ci * 512:(ci + 1) * 512]
                    ssel = lambda p_: p_[:, :]
                elif d == 4:
                    dsel = lambda buf: buf.rearrange("p (i d) -> p d i", d=4)[:, ci, :]
                    ssel = lambda p_: p_[:, :]
                else:
                    dsel = lambda buf: buf.rearrange("p (i d) -> p d i", d=16)[:, 4 * ci:4 * ci + 4, :]
                    ssel = lambda p_: p_[:, :].rearrange("p (r i) -> p r i", r=4)
                if out_mode == "B":
                    R = TMPF[:, 2 + ck % 2, :]
                    Rb = TMPFb[2 + ck % 2]
                    dve([I("tensor_scalar", out=R, in0=PS[bd][:, :], scalar1=ESINK[:, sink_col:sink_col + 1], scalar2=None, op0=ALU.add)],
                        [PSb[bd], ESINKb], [Rb])
                    dve([I("reciprocal", out=R, in_=R)], [Rb], [Rb])
                    dve([I("tensor_tensor", out=dsel(OST[oi]), in0=ssel(PS[bo]), in1=R, op=ALU.mult)], [PSb[bo], Rb], [ostb[oi]])
                else:
                    Obuf = XT[:, gidx, :]
                    Dsum = XT[:, 3, :]
                    act([I("activation", out=dsel(Obuf), in_=ssel(PS[bo]), func=AF.Copy)], [PSb[bo]], [XTb[gidx]])
                    if gidx == 0:
                        dve([I("tensor_copy", out=dsel(Dsum), in_=ssel(PS[bd]))], [PSb[bd]], [XTb[3]])
                    else:
                        dve([I("tensor_tensor", out=dsel(Dsum), in0=ssel(PS[bd]), in1=dsel(Dsum), op=ALU.add)], [PSb[bd], XTb[3]], [XTb[3]])

            emit_qk(descs[0])
            emit_qk(descs[1])
            for i_, dc in enumerate(descs):
                if i_ + 2 < len(descs):
                    emit_qk(descs[i_ + 2])
                emit_rest(dc)
                if dc["qq"] == 3:
                    emit_evac(dc["ci"], dc["ck"], dc["bo"], dc["bd"])
            return oi

        def load_kv(s, ksrc, vsrc, vcol0, d, offs):
            p0 = s * 2048
            nt = len(offs)
            off0 = offs[0]
            H = -off0 * d
            ki = cnt["kv"] % 2
            cnt["kv"] += 1
            glo = max(0, (p0 - H) // G)
            ghi = min(NG - 1, (p0 + 2048 + H - 1) // G)
            rd = [p1b[g] for g in range(glo, ghi + 1)] + [padb]
            tr.dma("sp", [(KT[ki][:, 0:2048 + 2 * H], ksrc[:, PAD + p0 - H:PAD + p0 + 2048 + H])], rd, [ktb[ki]])
            nqb = 2048 // (128 * d)
            ntm = nqb + nt - 1
            prs = []
            for r in range(d):
                row0 = PAD + p0 + r + d * off0
                rows = vsrc[row0:row0 + ntm * 128 * d, vcol0:vcol0 + 128]
                src = rows.rearrange("(m k dd) c -> dd k m c", k=128, dd=d)[0]
                prs.append((VT[ki][:, r * ntm:(r + 1) * ntm, :], src))
            tr.dma("sp", prs, rd, [vtb[ki]])
            return ki

        for s in range(4):
            p0 = s * 2048
            for kv in range(2):
                ki = load_kv(s, kbT[kv], vb, kv * 128, 1, (-128, 0, 128))
                for hh in range(2):
                    h = 2 * kv + hh
                    oi = run_head(s, qbT[h], 1, (-128, 0, 128), h, l * 4 + h, "B", 0, ki)
                    tr.dma("pool", [(mixbc[h, :, p0:p0 + 2048], OST[oi])], [ostb[oi]], [mixb[s]])
            for j in range(2):
                for gi, d in enumerate((1, 4, 16)):
                    ki = load_kv(s, kcT[gi], vc, gi * 128, d, (-64, 64))
                    hc = 2 * gi + j
                    run_head(s, qcT[hc], d, (-64, 64), 4 + hc, None, "C", gi, ki)
                Dsum = XT[:, 3, :]
                dve([I("reciprocal", out=Dsum, in_=Dsum)], [XTb[3]], [XTb[3]])
                for gi in range(3):
                    oi = cnt["o"] % 2
                    cnt["o"] += 1
                    dve([I("tensor_tensor", out=OST[oi], in0=XT[:, gi, :], in1=Dsum, op=ALU.mult)], [XTb[gi], XTb[3]], [ostb[oi]])
                    tr.dma("pool", [(mixbc[4 + 2 * gi + j, :, p0:p0 + 2048], OST[oi])], [ostb[oi]], [mixb[s]])

    def phase2b(l):
        arena.begin()
        xsrc = x_in if l == 0 else xres
        KAT = BIG[:, 0:16384].rearrange("p (h t) -> p h t", h=2)
        VA = BIG[:, 16384:32768].rearrange("p (m c) -> p m c", m=64)
        katb, vab = arena.buf(), arena.buf()
        QA = MISC[:, 0:3072].rearrange("p (h t) -> p h t", h=6)
        qab = arena.buf()
        TMPFB = TMPF.bitcast(BF16)
        PTm = [MISC[:, 3072:4096], MISC[:, 4096:5120], TMPFB[:, 3, :]]
        PTSm = [TMPFB[:, 1, 0:512], TMPFB[:, 1, 512:1024], TMPFB[:, 2, 0:512], TMPFB[:, 2, 512:1024]]

        def alias_buf(srcs):
            b_ = arena.buf()
            for sb_ in srcs:
                if sb_.w is not None and b_.r.get(sb_.w[0], 0) < sb_.w[1]:
                    b_.r[sb_.w[0]] = sb_.w[1]
                for s_, v_ in sb_.r.items():
                    if b_.r.get(s_, 0) < v_:
                        b_.r[s_] = v_
            return b_
        PTb = [arena.buf(), arena.buf(), alias_buf([TMPFb[3]])]
        PTSb = [alias_buf([TMPFb[1]]), alias_buf([TMPFb[1]]), alias_buf([TMPFb[2]]), alias_buf([TMPFb[2]])]
        S = wsc[l]
        tr.dma("sp", [(KAT[:, h, :], kaT[h]) for h in range(2)], p1b, [katb])
        tr.dma("sp", [(VA[:, :, :], va.rearrange("(m p) c -> p m c", p=128))], p1b, [vab])
        ODS = [((PS[4], PSb[4]), (PS[5], PSb[5])), ((PTBF[0], PTBb[0]), (PTBF[1], PTBb[1]))]
        npair = 0
        tr.dma("sp", [(QA, qaT[:, :, 0:G].rearrange("h p t -> p h t"))], [p1b[0]], [qab])
        for g in range(NG):
            g0 = g * G
            half = g // 8
            pre_w = [wload(S["out_tm"][0, seg], wbuf[l]["out_tm"], (8, 512)) for seg in range(2)]
            for h in range(6):
                kv = h // 3
                (Oap, Ob_), (Dap, Db_) = ODS[h % 2]

                def st_(p):
                    pr = p % 2
                    pe([MM(PSP[pr][:, 0:512], KAT[:, kv, (2 * p) * 128:(2 * p + 1) * 128], QA[:, h, :], True, True),
                        MM(PSP[pr][:, 512:1024], KAT[:, kv, (2 * p + 1) * 128:(2 * p + 2) * 128], QA[:, h, :], True, True)],
                       [katb, qab], [PSb[2 * pr], PSb[2 * pr + 1]])
                st_(0)
                bufs = {}
                for p in range(34):
                    if p + 1 < 32:
                        st_(p + 1)
                    if p < 32:
                        pr = p % 2
                        pt, ptb = PTm[npair % 3], PTb[npair % 3]
                        pts, ptsb = PTSm[npair % 4], PTSb[npair % 4]
                        npair += 1
                        bufs[p] = (pt, ptb, pts, ptsb)
                        if (2 * p) // 32 != half:
                            act([I("activation", out=pt, in_=PSP[pr][:, :], func=AF.Exp, bias=FLAGS[:, 1:2], scale=SCALE)],
                                [PSb[2 * pr], PSb[2 * pr + 1], CONSTb], [ptb])
                        else:
                            act([I("activation", out=pt, in_=PSP[pr][:, :], func=AF.Exp, scale=SCALE)], [PSb[2 * pr], PSb[2 * pr + 1]], [ptb])
                        dve([I("tensor_tensor", out=pts, in0=pt[:, 0:512], in1=pt[:, 512:1024], op=ALU.add)], [ptb], [ptsb])
                    if 1 <= p <= 32:
                        q = p - 1
                        pt, ptb, _, _ = bufs[q]
                        ins = []
                        for q2 in range(2):
                            kt = 2 * q + q2
                            ins.append(MM(Oap[:, :], VA[:, kt, kv * 128:(kv + 1) * 128], pt[:, q2 * 512:(q2 + 1) * 512], kt == 0, kt == 63))
                        pe(ins, [ptb, vab], [Ob_])
                    if 2 <= p <= 33:
                        q = p - 2
                        _, _, pts, ptsb = bufs[q]
                        pe([MM(Dap[:, :], ONESB[:, :], pts, q == 0, q == 31)], [ptsb, CONSTb], [Db_])
                R = TMPF[:, 0, :]
                Rb = TMPFb[0]
                dve([I("reciprocal", out=R, in_=Dap[:, :])], [Db_], [Rb])
                dve([I("tensor_tensor", out=HT[:, h, :], in0=Oap[:, :], in1=R, op=ALU.mult)], [Ob_, Rb], HTb)
            if g + 1 < NG:
                tr.dma("sp", [(QA, qaT[:, :, g0 + G:g0 + 2 * G].rearrange("h p t -> p h t"))], [p1b[g + 1]], [qab])
            tr.dma("sp", [(HT[:, 6:16, :], mixbc[:, :, g0:g0 + G].rearrange("h p t -> p h t"))], [mixb[g // 4]], HTb)
            for nb in range(4):
                for seg in range(2):
                    if nb == 0:
                        wv, wb_ = pre_w[seg]
                    else:
                        wv, wb_ = wload(S["out_tm"][nb, seg], wbuf[l]["out_tm"], (8, 512))
                    for t in range(4):
                        bk = (4 * nb + t) % 6
                        tm_matmul(PS[bk][:, :], PSb[bk], lambda c, t=t: HT[:, c, t * 128:(t + 1) * 128], [HTb[t]], wv, wb_, 512, seg * 8, 8, 16)
                for t in range(4):
                    bk = (4 * nb + t) % 6
                    tt = 4 * g + t
                    xs = XT[:, t, nb * 512:(nb + 1) * 512]
                    tr.dma("sp", [(xs, xsrc[tt * 128:(tt + 1) * 128, nb * 512:(nb + 1) * 512])], [xb[g]], [XTb[t]])
                    dve([I("tensor_tensor", out=xs, in0=PS[bk][:, :], in1=xs, op=ALU.add)], [PSb[bk], XTb[t]], [XTb[t]])
            for t in range(4):
                tt = 4 * g + t
                tr.dma("pool", [(xres[tt * 128:(tt + 1) * 128, :], XT[:, t, :])], [XTb[t]], [xb[g]])
        for ab_, k_ in ((PTb[2], 3), (PTSb[0], 1), (PTSb[1], 1), (PTSb[2], 2), (PTSb[3], 2)):
            evs = list(ab_.r.items()) + ([ab_.w] if ab_.w is not None else [])
            for s_, v_ in evs:
                if TMPFb[k_].r.get(s_, 0) < v_:
                    TMPFb[k_].r[s_] = v_

    def phase3(l, conv_chunks):
        arena.begin()
        S = wsc[l]
        AT = BIG[:, 0:22528].rearrange("p (j t) -> p j t", j=44)
        KXTs = [BIG[:, 26624:27648].rearrange("p (h m) -> p h m", h=4), MISC[:, 2048:3072].rearrange("p (h m) -> p h m", h=4)]
        VXs = [BIG[:, 27648:28672].rearrange("p (m c) -> p m c", m=2), MISC[:, 3072:4096].rearrange("p (m c) -> p m c", m=2)]
        QX = BIG[:, 28672:30720].rearrange("p (h t) -> p h t", h=4)
        OX = BIG[:, 30720:32768].rearrange("p (h t) -> p h t", h=4)
        atb = [arena.buf() for i in range(6)]
        kxbs, vxbs = [arena.buf(), arena.buf()], [arena.buf(), arena.buf()]
        qxb, oxb = arena.buf(), arena.buf()
        HB = [BIG[:, 22528:24576], BIG[:, 24576:26624], BIG[:, 28672:30720], BIG[:, 30720:32768]]
        HBb = [arena.buf(), arena.buf(), qxb, oxb]
        PTm = [MISC[:, i * 512:(i + 1) * 512] for i in range(4)]
        PTb = [arena.buf() for _ in range(4)]
        npt = 0
        load_gain(GB, GBb, ln_ffn[l:l + 1, :])
        load_gain(GA, GAb, ln_mem[l:l + 1, :])
        for half in range(2):
            for mt in range(2):
                tr.dma("sp", [(XT[:, mt, :], mem_in[half * 256 + mt * 128:half * 256 + (mt + 1) * 128, :])], [], [XTb[mt]])
            norm_stage_a([XT[:, mt, :] for mt in range(2)], XTb, GA, GAb, HB, HBb)
            norm_stage_b(2, HB, HBb, HT, HTb)
            for b in range(2):
                wv, wb_ = wload(S["ckv_fm"][b], wbuf[l]["ckv_fm"], (16, 256))
                for jj in range(2):
                    h = 2 * b + jj
                    bk = h % 6
                    fm_matmul(PS[bk][:, 0:256], PSb[bk], wv, wb_, jj * 128, lambda c: HT[:, c, 0:256], [HTb[0], HTb[1]])
                    copy_any(KXTs[half][:, h, :], PS[bk][:, 0:256], [PSb[bk]], [kxbs[half]])
            for seg in range(2):
                wv, wb_ = wload(S["ckv_tm"][0, seg], wbuf[l]["ckv_tm"], (8, 512))
                for mt in range(2):
                    tm_matmul(PS[4 + mt][:, :], PSb[4 + mt], lambda c, mt=mt: HT[:, c, mt * 128:(mt + 1) * 128], [HTb[mt]], wv, wb_, 512, seg * 8, 8, 16)
            for mt in range(2):
                copy_any(VXs[half][:, mt, :], PS[4 + mt][:, :], [PSb[4 + mt]], [vxbs[half]])
        load_gain(GA, GAb, ln_cross[l:l + 1, :])

        def stage_x(g):
            for t in range(4):
                tt = 4 * g + t
                tr.dma("sp", [(XT[:, t, :], xres[tt * 128:(tt + 1) * 128, :])], [xb[g]], [XTb[t]])
            norm_stage_a([XT[:, t, :] for t in range(4)], XTb, GA, GAb, HB, HBb)

        def stage_xb(g):
            norm_stage_b(4, HB, HBb, HT, HTb)

        stage_x(0)
        stage_xb(0)
        for g in range(NG):
            g0 = g * G
            half = g // 8
            KXT, VX, kxb, vxb = KXTs[half], VXs[half], kxbs[half], vxbs[half]
            cpieces = []
            if conv_chunks and g < len(conv_chunks):
                for (lk, key, prs) in conv_chunks[g]:
                    for i_ in range(0, len(prs), 2):
                        cpieces.append((lk, key, prs[i_:i_ + 2]))

            def emit_conv_pieces(k, n):
                m = (len(cpieces) + n - 1) // n
                for (lk, key, prs) in cpieces[k * m:(k + 1) * m]:
                    tr.dma("pool", prs, reads=[], writes=[wbuf[lk][key]])
            emit_conv_pieces(0, 6)
            for b in range(2):
                wv, wb_ = wload(S["cq_fm"][b], wbuf[l]["cq_fm"], (16, 256))
                for jj in range(2):
                    h = 2 * b + jj
                    bk = h % 2
                    fm_matmul(PS[bk][:, :], PSb[bk], wv, wb_, jj * 128, lambda c: HT[:, c, :], HTb)
                    copy_any(QX[:, h, :], PS[bk][:, :], [PSb[bk]], [qxb])
            for h in range(4):
                bo, bd = 2 + 2 * (h % 2), 3 + 2 * (h % 2)
                pts = []
                for mt in range(2):
                    bs = mt
                    pe([MM(PS[bs][:, :], KXT[:, h, mt * 128:(mt + 1) * 128], QX[:, h, :], True, True)], [kxb, qxb], [PSb[bs]])
                    pt = PTm[npt % 4]
                    ptb = PTb[npt % 4]
                    npt += 1
                    act([I("activation", out=pt, in_=PS[bs][:, :], func=AF.Exp, scale=SCALE)], [PSb[bs]], [ptb])
                    pts.append((pt, ptb))
                ins = [MM(PS[bo][:, :], VX[:, mt, h * 128:(h + 1) * 128], pts[mt][0], mt == 0, mt == 1) for mt in range(2)]
                ins += [MM(PS[bd][:, :], ONESB[:, :], pts[mt][0], mt == 0, mt == 1) for mt in range(2)]
                pe(ins, [pts[0][1], pts[1][1], vxb, CONSTb], [PSb[bo], PSb[bd]])
                R = TMPF[:, h % 2, :]
                Rb = TMPFb[h % 2]
                dve([I("reciprocal", out=R, in_=PS[bd][:, :])], [PSb[bd]], [Rb])
                dve([I("tensor_tensor", out=OX[:, h, :], in0=PS[bo][:, :], in1=R, op=ALU.mult)], [PSb[bo], Rb], [oxb])
            for nb in range(4):
                wv, wb_ = wload(S["co_tm"][nb, 0], wbuf[l]["co_tm"], (8, 512))
                for t in range(4):
                    bk = (4 * nb + t) % 6
                    tm_matmul(PS[bk][:, :], PSb[bk], lambda c, t=t: OX[:, c, t * 128:(t + 1) * 128], [oxb], wv, wb_, 512, 0, 4, 4)
                    xs = XT[:, t, nb * 512:(nb + 1) * 512]
                    dve([I("tensor_tensor", out=xs, in0=PS[bk][:, :], in1=xs, op=ALU.add)], [PSb[bk], XTb[t]], [XTb[t]])
            for t in range(4):
                tt = 4 * g + t
                tr.dma("pool", [(xres[tt * 128:(tt + 1) * 128, :], XT[:, t, :])], [XTb[t]], [xb[g]])
            emit_conv_pieces(1, 6)
            norm_stage_a([XT[:, t, :] for t in range(4)], XTb, GB, GBb, HB, HBb)
            norm_stage_b(4, HB, HBb, HT, HTb)
            for j in range(44):
                wv, wb_ = wload(S["ffi_fm"][j], wbuf[l]["ffi_fm"], (16, 256))
                bg, bu = 2 * (j % 3), 2 * (j % 3) + 1
                fm_matmul(PS[bg][:, :], PSb[bg], wv, wb_, 0, lambda c: HT[:, c, :], HTb)
                fm_matmul(PS[bu][:, :], PSb[bu], wv, wb_, 128, lambda c: HT[:, c, :], HTb)
                sg = TMPF[:, 2 + j % 2, :]
                sgb = TMPFb[2 + j % 2]
                act([I("activation", out=sg, in_=PS[bg][:, :], func=AF.Silu)], [PSb[bg]], [sgb])
                dve([I("tensor_tensor", out=AT[:, j, :], in0=PS[bu][:, :], in1=sg, op=ALU.mult)], [PSb[bu], sgb], [atb[j // 8]])
            if g + 1 < NG:
                stage_x(g + 1)
            for t in range(4):
                tt = 4 * g + t
                tr.dma("pool", [(TMPF[:, t, :], xres[tt * 128:(tt + 1) * 128, 0:512])], [xb[g]], [TMPFb[t]])
            for nb in range(4):
                for seg in range(6):
                    nch = 8 if seg < 5 else 4
                    wv, wb_ = wload(S["ffo_tm"][nb, seg], wbuf[l]["ffo_tm"], (8, 512))
                    for t in range(4):
                        bk = (4 * nb + t) % 6
                        tm_matmul(PS[bk][:, :], PSb[bk], lambda c, t=t: AT[:, c, t * 128:(t + 1) * 128], [atb[seg]], wv, wb_, 512, seg * 8, nch, 44)
                if nb == 1 and g + 1 < NG:
                    stage_xb(g + 1)
                for t in range(4):
                    bk = (4 * nb + t) % 6
                    tt = 4 * g + t
                    xs = TMPF[:, t, :]
                    dve([I("tensor_tensor", out=xs, in0=PS[bk][:, :], in1=xs, op=ALU.add)], [PSb[bk], TMPFb[t]], [TMPFb[t]])
                    tr.dma("pool", [(xres[tt * 128:(tt + 1) * 128, nb * 512:(nb + 1) * 512], xs)], [TMPFb[t]], [xb[g]])
                    if nb + 1 < 4:
                        tr.dma("pool", [(xs, xres[tt * 128:(tt + 1) * 128, (nb + 1) * 512:(nb + 2) * 512])], [xb[g]], [TMPFb[t]])
                emit_conv_pieces(2 + nb, 6)

    def phase4():
        arena.begin()
        load_gain(GB, GBb, ln_final[0:1, :])
        jbs = [arena.buf(), arena.buf()]
        for tt in range(T // 128):
            t = tt % 4
            g = tt // 4
            xt = XT[:, t, :]
            tr.dma("sp", [(xt, xres[tt * 128:(tt + 1) * 128, :])], [xb[g]], [XTb[t]])
            sv, sbuf_ = stat_slot()
            junk = BIG[:, (tt % 2) * 2048:(tt % 2 + 1) * 2048]
            jb = jbs[tt % 2]
            act([I("activation", out=junk, in_=xt, func=AF.Square, accum_out=sv[:, 0:1])], [XTb[t]], [jb, sbuf_])
            act([I("activation", out=sv[:, 1:2], in_=sv[:, 0:1], func=AF.Sqrt, bias=EPS, scale=1.0 / D)], [sbuf_], [sbuf_])
            dve([I("reciprocal", out=sv[:, 2:3], in_=sv[:, 1:2])], [sbuf_], [sbuf_])
            dve([I("scalar_tensor_tensor", out=xt, in0=xt, scalar=sv[:, 2:3], in1=GB[:, :], op0=ALU.mult, op1=ALU.mult)],
                [XTb[t], sbuf_, GBb], [XTb[t]])
            tr.dma("pool", [(y_out[tt * 128:(tt + 1) * 128, :], xt)], [XTb[t]], [xb[g]], is_output=True)

    for key, prs in conv_ops(0):
        emit_conv(0, key, prs)
    setup()
    for l in range(nlayers):
        phase1(l)
        phase2a(l)
        phase2b(l)
        chunks = None
        if l + 1 < nlayers:
            allp = []
            for key, prs in conv_ops(l + 1):
                for i in range(0, len(prs), 16):
                    allp.append((l + 1, key, prs[i:i + 16]))
            per = (len(allp) + NG - 1) // NG
            chunks = [allp[i * per:(i + 1) * per] for i in range(NG)]
        phase3(l, chunks)
    phase4()
    build.stats = {k: len(v) for k, v in tr.ops.items()}

    with nc.Block() as block:
        @block.sync
        def _(e):
            tr.replay("sp", e, sems)

        @block.scalar
        def _(e):
            tr.replay("act", e, sems)

        @block.tensor
        def _(e):
            tr.replay("pe", e, sems)

        @block.vector
        def _(e):
            tr.replay("dve", e, sems)

        @block.gpsimd
        def _(e):
            tr.replay("pool", e, sems)
            done = {}
            for s, v in tr.out_events:
                done[s] = max(done.get(s, 0), v)
            for s, v in done.items():
                e.wait_ge(sems[s], v)
    st.close()
    return nc


def _t5_bucket(rel):
    nb = 16
    max_exact = 8
    ret = np.where(rel > 0, nb, 0)
    n = np.abs(rel)
    large = max_exact + (np.log(np.maximum(n, 1).astype(np.float32) / max_exact)
                         / np.float32(math.log(1024 / max_exact)) * (nb - max_exact)).astype(np.int32)
    large = np.minimum(large, nb - 1)
    return ret + np.where(n < max_exact, n, large)


def _rope_table(positions):
    inv = (10000.0 ** (-np.arange(0, 64, 2, dtype=np.float32) / 64)).astype(np.float32)
    row = (positions // 64).astype(np.float32)
    col = (positions % 64).astype(np.float32)
    ar = row[:, None] * inv
    ac = col[:, None] * inv
    cr, sr, cc, sc = np.cos(ar), np.sin(ar), np.cos(ac), np.sin(ac)
    C = np.concatenate([cr, cr, cc, cc], axis=1)
    S = np.concatenate([-sr, sr, -sc, sc], axis=1)
    return np.concatenate([C, S], axis=1).astype(np.float32)


def _host_tables():
    m = np.arange(512)
    rel = 256 - m
    oh = np.zeros((32, 3, 512), np.float32)
    for gt, d in enumerate((1, 4, 16)):
        bk = _t5_bucket((rel * d).astype(np.int32))
        oh[bk, gt, m] = 1.0
    mk = np.zeros((10, 512), np.float32)
    mk[0:4] = (np.abs(rel) <= 128).astype(np.float32)
    mk[4:10] = (np.abs(rel) <= 64).astype(np.float32)
    ident = np.eye(128, dtype=np.float32)
    jmat = np.ascontiguousarray(ident[::-1])
    return oh.reshape(32, 1536), mk, ident, jmat


def _flags(cross):
    p = np.arange(128)
    f = np.zeros((128, 8), np.float32)
    mlo = (p >= 64).astype(np.float32)
    mhi = (p < 64).astype(np.float32)
    f[:, 0] = cross
    f[:, 1] = 0.0 if cross else -30000.0
    f[:, 2] = mlo
    f[:, 3] = mhi
    f[:, 4] = np.maximum(mlo, cross)
    f[:, 5] = np.maximum(mhi, cross)
    return f


_NC_CACHE = {}


def make_in_maps(inputs):
    f32 = lambda a: np.ascontiguousarray(np.asarray(a, dtype=np.float32))
    xp, xs = f32(inputs["x_prompt"]), f32(inputs["x_sample"])
    mp, ms = f32(inputs["mem_prompt"]), f32(inputs["mem_sample"])
    oh, mk, ident, jmat = _host_tables()
    shared = {k: f32(inputs[k]) for k in ("ln_mix", "w_in", "q_norm_a", "k_norm_a", "rel_bias", "w_out", "ln_cross", "ln_mem",
                                         "w_cq", "w_ckv", "w_co", "ln_ffn", "w_ffn_in", "w_ffn_out")}
    shared["sink_b"] = f32(inputs["sink_b"]).reshape(1, 16)
    shared["ln_final"] = f32(inputs["ln_final"]).reshape(1, D)
    shared.update(oh=oh, mk=mk, ident=ident, jmat=jmat)
    rope_p = _rope_table(np.arange(T))
    rope_s = _rope_table(np.concatenate([np.arange(HALF), np.arange(HALF)]))
    maps = []
    for c in range(8):
        m = dict(shared)
        if c < 2:
            m["x"] = xp[c]
            m["mem"] = np.concatenate([mp[c], mp[c]], axis=0)
            m["rope"] = rope_p
            m["flags"] = _flags(1.0)
        elif c < 6:
            i = 2 * (c - 2)
            m["x"] = np.concatenate([xs[i], xs[i + 1]], axis=0)
            m["mem"] = np.concatenate([ms[i], ms[i + 1]], axis=0)
            m["rope"] = rope_s
            m["flags"] = _flags(0.0)
        else:
            m["x"] = np.zeros((T, D), np.float32)
            m["mem"] = np.zeros((512, D), np.float32)
            m["rope"] = rope_s
            m["flags"] = _flags(0.0)
        maps.append(m)
    return maps


def kernel(**inputs):
    if "nc" not in _NC_CACHE:
        _NC_CACHE["nc"] = build()
    nc = _NC_CACHE["nc"]
    maps = make_in_maps(inputs)
    res = run_bass_kernel_spmd(nc, maps, core_ids=list(range(8)))
    ys = [np.asarray(r["y"], dtype=np.float32) for r in res.results]
    y_prompt = np.stack([ys[0], ys[1]], axis=0)
    y_sample = np.stack([ys[2 + i // 2][(i % 2) * HALF:(i % 2 + 1) * HALF] for i in range(8)], axis=0)
    return (y_prompt, y_sample)
```
